# Optimizing a Trainium2 kernel written in Bass

```python
import jax, jax.numpy as jnp
from jax import lax
import numpy as np


D_MODEL = 1024
BATCH = 1
SEQ = 16384
DEPTH = 2
DEC_BATCH = 16
DEC_SEQ = 32
PAST_LEN = 4096

CHUNK = 64
HEAD_DIM = 64
ATTN_SCALE = HEAD_DIM ** -0.5
POOL_WINDOWS = (2, 4, 8, 16)
N_POOL_GROUPS = 4
C_POOL = D_MODEL // 2
POOL_GROUP = C_POOL // N_POOL_GROUPS
POOL_PAD = 15
SWA_HEADS = (D_MODEL // 2) // HEAD_DIM
SWA_KV_HEADS = 2
SWA_REP = SWA_HEADS // SWA_KV_HEADS
WINDOW = 128
SWA_NB = WINDOW // CHUNK
MIX_AB = C_POOL + SWA_HEADS * HEAD_DIM
AB_IN = MIX_AB + 2 * SWA_KV_HEADS * HEAD_DIM + MIX_AB
FOX_HEADS = D_MODEL // HEAD_DIM
MIX_C = FOX_HEADS * HEAD_DIM
C_IN = 4 * MIX_C + FOX_HEADS
QBLK = 128
FORGET_BIAS_INIT = 3.0
NORM_EPS = 1e-6
NEG_INF = -1e30
F32 = jnp.float32

kernel_name = 'chunk_causal_pool_swa_fox_hybrid_step'


def rms_norm(x, g):
    xf = x.astype(F32)
    y = xf * lax.rsqrt(jnp.mean(xf * xf, axis=-1, keepdims=True) + NORM_EPS)
    return (y * g.astype(F32)).astype(x.dtype)


def alibi_slopes():
    s = 2.0 ** (-(8.0 / SWA_HEADS) * np.arange(1, SWA_HEADS + 1))
    return jnp.asarray(s.astype(np.float32)).reshape(SWA_KV_HEADS, SWA_REP, 1, 1)


def multiscale_pool(u, prefix, pos, w_pool, pool_scale):
    n, t, _ = u.shape
    ext = jnp.concatenate([prefix.astype(u.dtype), u], axis=1)
    cs = jnp.pad(jnp.cumsum(ext.astype(F32), axis=1), ((0, 0), (1, 0), (0, 0)))
    outs = []
    for g, w in enumerate(POOL_WINDOWS):
        sl = slice(g * POOL_GROUP, (g + 1) * POOL_GROUP)
        hi = cs[:, POOL_PAD + 1:POOL_PAD + 1 + t, sl]
        lo = cs[:, POOL_PAD + 1 - w:POOL_PAD + 1 - w + t, sl]
        cnt = jnp.minimum(pos + 1, w).astype(F32)[None, :, None]
        outs.append((hi - lo) / cnt)
    pooled = jnp.stack(outs, axis=2)
    diff = (pooled - u.reshape(n, t, N_POOL_GROUPS, POOL_GROUP).astype(F32)).astype(u.dtype)
    mixed = jnp.einsum('ntgc,gcd->ntgd', diff, w_pool).reshape(n, t, C_POOL)
    return mixed * pool_scale, ext[:, -POOL_PAD:]


def sink_attention(q, k, v, bias, sinks):
    s = jnp.einsum('...qgrd,...kgd->...grqk', q, k).astype(F32) * ATTN_SCALE + bias
    sink = sinks.astype(F32).reshape(SWA_KV_HEADS, SWA_REP, 1, 1)
    m = jnp.maximum(jnp.max(s, axis=-1, keepdims=True), sink)
    p = jnp.exp(s - m)
    w = p / (jnp.sum(p, axis=-1, keepdims=True) + jnp.exp(sink - m))
    return jnp.einsum('...grqk,...kgd->...qgrd', w.astype(v.dtype), v)


def swa_prompt(q, k, v, sinks):
    n, t = q.shape[:2]
    nc = t // CHUNK
    qc = q.reshape(n, nc, CHUNK, SWA_KV_HEADS, SWA_REP, HEAD_DIM)

    def band(a):
        ac = a.reshape(n, nc, CHUNK, SWA_KV_HEADS, HEAD_DIM)
        ap = jnp.pad(ac, ((0, 0), (SWA_NB, 0), (0, 0), (0, 0), (0, 0)))
        return jnp.concatenate([ap[:, j:j + nc] for j in range(SWA_NB + 1)], axis=2)

    kb, vb = band(k), band(v)
    kj = jnp.arange((SWA_NB + 1) * CHUNK)
    rel = jnp.arange(CHUNK)[:, None] + SWA_NB * CHUNK - kj[None, :]
    alibi = -alibi_slopes() * jnp.abs(rel).astype(F32)
    kpos = (jnp.arange(nc)[:, None] - SWA_NB) * CHUNK + kj[None, :]
    mask = jnp.where(kpos >= 0, 0.0, NEG_INF).astype(F32)[:, None, None, None, :]
    out = sink_attention(qc, kb, vb, alibi + mask, sinks)
    return out.reshape(n, t, SWA_HEADS * HEAD_DIM)


def swa_sample(q, k, v, k_prefix, v_prefix, sinks):
    n, t = q.shape[:2]
    kk = jnp.concatenate([k_prefix.astype(k.dtype), k], axis=1)
    vv = jnp.concatenate([v_prefix.astype(v.dtype), v], axis=1)
    rel = jnp.arange(t)[:, None] + WINDOW - jnp.arange(WINDOW + t)[None, :]
    bias = -alibi_slopes() * jnp.abs(rel).astype(F32)
    out = sink_attention(q, kk, vv, bias, sinks)
    return out.reshape(n, t, SWA_HEADS * HEAD_DIM), kk[:, -WINDOW:], vv[:, -WINDOW:]


def ab_layer(x, pos, pool_prefix, swa_k_prefix, swa_v_prefix, norm_g, w_in, w_pool, pool_scale, qn_g, kn_g, sinks, w_out):
    n, t, _ = x.shape
    h = rms_norm(x, norm_g)
    proj = h @ w_in
    i1 = C_POOL
    i2 = i1 + SWA_HEADS * HEAD_DIM
    i3 = i2 + SWA_KV_HEADS * HEAD_DIM
    i4 = i3 + SWA_KV_HEADS * HEAD_DIM
    u, q, k, v, gate = jnp.split(proj, [i1, i2, i3, i4], axis=-1)
    pool_out, pool_state = multiscale_pool(u, pool_prefix, pos, w_pool, pool_scale)
    q = rms_norm(q.reshape(n, t, SWA_HEADS, HEAD_DIM), qn_g).reshape(n, t, SWA_KV_HEADS, SWA_REP, HEAD_DIM)
    k = rms_norm(k.reshape(n, t, SWA_KV_HEADS, HEAD_DIM), kn_g)
    v = v.reshape(n, t, SWA_KV_HEADS, HEAD_DIM)
    if swa_k_prefix is None:
        attn = swa_prompt(q, k, v, sinks)
        k_state, v_state = k[:, -WINDOW:], v[:, -WINDOW:]
    else:
        attn, k_state, v_state = swa_sample(q, k, v, swa_k_prefix, swa_v_prefix, sinks)
    mixed = jnp.concatenate([pool_out, attn], axis=-1) * jax.nn.silu(gate)
    return x + mixed @ w_out, pool_state, k_state, v_state


def forgetting_core(q, k, v, bias, mask):
    s = jnp.einsum('nqhd,nkhd->nhqk', q, k).astype(F32) * ATTN_SCALE + bias
    p = jax.nn.softmax(jnp.where(mask, s, -jnp.inf), axis=-1)
    return jnp.einsum('nhqk,nkhd->nqhd', p.astype(v.dtype), v)


def fox_prompt(q, k, v, logf):
    n, t = q.shape[:2]
    nb = t // QBLK
    cum_t = jnp.cumsum(logf, axis=1).transpose(0, 2, 1)
    kpos = jnp.arange(t)
    qb = q.reshape(n, nb, QBLK, FOX_HEADS, HEAD_DIM).swapaxes(0, 1)
    cb = cum_t.reshape(n, FOX_HEADS, nb, QBLK).transpose(2, 0, 1, 3)
    pb = kpos.reshape(nb, QBLK)

    def block(args):
        qi, ci, pi = args
        bias = ci[..., None] - cum_t[:, :, None, :]
        return forgetting_core(qi, k, v, bias, pi[:, None] >= kpos[None, :])

    out = lax.map(block, (qb, cb, pb))
    return out.swapaxes(0, 1).reshape(n, t, MIX_C)


def fox_sample(q, k, v, logf, k_prefix, v_prefix, logf_prefix):
    n, t = q.shape[:2]
    p_len = k_prefix.shape[1]
    kk = jnp.concatenate([k_prefix.astype(k.dtype), k], axis=1)
    vv = jnp.concatenate([v_prefix.astype(v.dtype), v], axis=1)
    cum_t = jnp.cumsum(jnp.concatenate([logf_prefix.astype(F32), logf], axis=1), axis=1).transpose(0, 2, 1)
    bias = cum_t[:, :, -t:, None] - cum_t[:, :, None, :]
    mask = (p_len + jnp.arange(t))[:, None] >= jnp.arange(p_len + t)[None, :]
    return forgetting_core(q, kk, vv, bias, mask).reshape(n, t, MIX_C)


def fox_layer(x, k_prefix, v_prefix, logf_prefix, norm_g, w_in, b_forget, qn_g, kn_g, w_out):
    n, t, _ = x.shape
    h = rms_norm(x, norm_g)
    proj = h @ w_in
    q, k, v, gate, fpre = jnp.split(proj, [MIX_C, 2 * MIX_C, 3 * MIX_C, 4 * MIX_C], axis=-1)
    q = rms_norm(q.reshape(n, t, FOX_HEADS, HEAD_DIM), qn_g)
    k = rms_norm(k.reshape(n, t, FOX_HEADS, HEAD_DIM), kn_g)
    v = v.reshape(n, t, FOX_HEADS, HEAD_DIM)
    logf = jax.nn.log_sigmoid(fpre.astype(F32) + b_forget.astype(F32))
    if k_prefix is None:
        attn = fox_prompt(q, k, v, logf)
    else:
        attn = fox_sample(q, k, v, logf, k_prefix, v_prefix, logf_prefix)
    return x + (attn * jax.nn.silu(gate)) @ w_out, k, v, logf


def setup_inputs(seed: int = 0) -> dict:
    key = jax.random.key(seed)
    ks = jax.random.split(key, 24)
    nrm = lambda i, shape: jax.random.normal(ks[i], shape, F32)
    return {
        'x_prompt': nrm(0, (BATCH, SEQ, D_MODEL)),
        'x_sample': nrm(1, (DEC_BATCH, DEC_SEQ, D_MODEL)),
        'state_pool': nrm(2, (DEC_BATCH, POOL_PAD, C_POOL)),
        'cache_swa_k': nrm(3, (DEC_BATCH, WINDOW, SWA_KV_HEADS, HEAD_DIM)),
        'cache_swa_v': nrm(4, (DEC_BATCH, WINDOW, SWA_KV_HEADS, HEAD_DIM)),
        'cache_fox_k': nrm(5, (DEC_BATCH, PAST_LEN, FOX_HEADS, HEAD_DIM)),
        'cache_fox_v': nrm(6, (DEC_BATCH, PAST_LEN, FOX_HEADS, HEAD_DIM)),
        'cache_fox_logf': jax.nn.log_sigmoid(FORGET_BIAS_INIT + nrm(7, (DEC_BATCH, PAST_LEN, FOX_HEADS))),
        'norm0_g': 1.0 + 0.02 * nrm(8, (D_MODEL,)),
        'w_in0': nrm(9, (D_MODEL, AB_IN)) * D_MODEL ** -0.5,
        'w_pool': nrm(10, (N_POOL_GROUPS, POOL_GROUP, POOL_GROUP)) * POOL_GROUP ** -0.5,
        'pool_scale': 1.0 + 0.1 * nrm(11, (C_POOL,)),
        'swa_qn_g': 1.0 + 0.02 * nrm(12, (HEAD_DIM,)),
        'swa_kn_g': 1.0 + 0.02 * nrm(13, (HEAD_DIM,)),
        'swa_sinks': 0.5 * nrm(14, (SWA_HEADS,)),
        'w_out0': nrm(15, (MIX_AB, D_MODEL)) * MIX_AB ** -0.5,
        'norm1_g': 1.0 + 0.02 * nrm(16, (D_MODEL,)),
        'w_in1': nrm(17, (D_MODEL, C_IN)) * D_MODEL ** -0.5,
        'b_forget': FORGET_BIAS_INIT + 0.5 * nrm(18, (FOX_HEADS,)),
        'fox_qn_g': 1.0 + 0.02 * nrm(19, (HEAD_DIM,)),
        'fox_kn_g': 1.0 + 0.02 * nrm(20, (HEAD_DIM,)),
        'w_out1': nrm(21, (MIX_C, D_MODEL)) * MIX_C ** -0.5,
    }


def reference(x_prompt, x_sample, state_pool, cache_swa_k, cache_swa_v, cache_fox_k, cache_fox_v, cache_fox_logf,
              norm0_g, w_in0, w_pool, pool_scale, swa_qn_g, swa_kn_g, swa_sinks, w_out0,
              norm1_g, w_in1, b_forget, fox_qn_g, fox_kn_g, w_out1):
    yp, ys = x_prompt, x_sample
    pos_p = jnp.arange(yp.shape[1])
    pos_s = PAST_LEN + jnp.arange(ys.shape[1])
    for layer in range(DEPTH):
        if layer % 2 == 0:
            zeros_prefix = jnp.zeros((yp.shape[0], POOL_PAD, C_POOL), yp.dtype)
            yp, pool_p, swk_p, swv_p = ab_layer(yp, pos_p, zeros_prefix, None, None, norm0_g, w_in0, w_pool,
                                                pool_scale, swa_qn_g, swa_kn_g, swa_sinks, w_out0)
            ys, pool_s, swk_s, swv_s = ab_layer(ys, pos_s, state_pool, cache_swa_k, cache_swa_v, norm0_g, w_in0,
                                                w_pool, pool_scale, swa_qn_g, swa_kn_g, swa_sinks, w_out0)
        else:
            yp, fk_p, fv_p, fl_p = fox_layer(yp, None, None, None, norm1_g, w_in1, b_forget, fox_qn_g, fox_kn_g, w_out1)
            ys, fk_s, fv_s, fl_s = fox_layer(ys, cache_fox_k, cache_fox_v, cache_fox_logf, norm1_g, w_in1, b_forget,
                                             fox_qn_g, fox_kn_g, w_out1)
    return (yp, ys, pool_p, pool_s, swk_p, swv_p, swk_s, swv_s, fk_p, fv_p, fl_p, fk_s, fv_s, fl_s)
```

```python
import numpy as np
import ml_dtypes
from contextlib import ExitStack
import concourse.bass as bass
import concourse.mybir as mybir
from concourse.bass_utils import run_bass_kernel_spmd

F32 = mybir.dt.float32
BF16 = mybir.dt.bfloat16
AF = mybir.ActivationFunctionType
ALU = mybir.AluOpType
AX = mybir.AxisListType

NCORES = 8
D = 1024
NT_FULL = 128
EPS = 1e-6
NEG = -30000.0
KA = 71
SYNC_SAME_ENGINE = True
KSTOP = 99.0


class Buf:
    __slots__ = ("name", "w", "r", "excl")

    def __init__(self, name, excl=False):
        self.name = name
        self.w = None
        self.r = []
        self.excl = excl


class Op:
    __slots__ = ("eng", "fn", "deps", "sig", "ticket", "dma", "slot", "idx")


class Prog:
    ENG = ("pe", "act", "dve", "pool", "sp")
    NRING = 12

    def __init__(self):
        self.ops = []
        self.ndma = 0
        self.ring_last = [None] * self.NRING
        self.capture = None

    def op(self, eng, fn, r=(), w=(), dma=False):
        if self.capture is not None:
            self.capture.append((eng, fn, tuple(r), tuple(w), dma))
            return None
        return self.commit(eng, fn, r, w, dma)

    def captured(self, emit_fn):
        assert self.capture is None
        self.capture = []
        emit_fn()
        out, self.capture = self.capture, None
        return out

    @staticmethod
    def interleave(A, B):
        out, i, j, na, nb = [], 0, 0, len(A), len(B)
        while i < na or j < nb:
            if j >= nb or (i < na and i * nb <= j * na):
                out.append(A[i]); i += 1
            else:
                out.append(B[j]); j += 1
        return out

    def commit(self, eng, fn, r=(), w=(), dma=False):
        o = Op()
        o.eng, o.fn, o.sig, o.ticket, o.dma, o.slot = eng, fn, False, None, dma, None
        o.idx = len(self.ops)
        deps = {}

        def add(d, kind):
            if d is not o:
                deps.setdefault(d, set()).add(kind)

        for b in r:
            if b in w:
                continue
            if b.w is not None:
                add(b.w, "RAW")
            if b.excl:
                for x in b.r:
                    add(x, "XRD")
        for b in w:
            if b.w is not None:
                add(b.w, "RAW" if b in r else "WAW")
            for x in b.r:
                add(x, "WAR")
        for b in r:
            if b not in w:
                b.r.append(o)
        for b in w:
            b.w = o
            b.r = []
        if dma:
            o.slot = self.ndma % self.NRING
            prev = self.ring_last[o.slot]
            if prev is not None:
                add(prev, "RING")
            self.ring_last[o.slot] = o
            self.ndma += 1
        o.deps = deps
        self.ops.append(o)
        return o

    def finalize(self):
        for o in self.ops:
            keep = set()
            for d, kinds in o.deps.items():
                if d.eng == o.eng and not d.dma:
                    if o.eng in ("pe", "sp"):
                        continue
                    if not SYNC_SAME_ENGINE:
                        continue
                    if o.eng != "pool" and not (kinds & {"RAW", "WAR"}):
                        continue
                d.sig = True
                keep.add(d)
            o.deps = keep
        cnt = {e: 0 for e in self.ENG}
        rc = [0] * self.NRING
        for o in self.ops:
            if o.dma:
                rc[o.slot] += 16
                o.ticket = rc[o.slot]
            elif o.sig:
                cnt[o.eng] += 1
                o.ticket = cnt[o.eng]

    def emit(self, nc, block, esem, dsem):
        per = {e: [o for o in self.ops if o.eng == e] for e in self.ENG}

        def run(ename, eng):
            waited = {}
            for o in per[ename]:
                need = {}
                for d in o.deps:
                    key = ("d", d.slot) if d.dma else ("e", d.eng)
                    if d.ticket > need.get(key, 0):
                        need[key] = d.ticket
                for key, val in need.items():
                    if waited.get(key, 0) >= val:
                        continue
                    waited[key] = val
                    sem = dsem[key[1]] if key[0] == "d" else esem[key[1]]
                    eng.wait_ge(sem, val)
                ins = o.fn(eng)
                if o.dma:
                    ins.then_inc(dsem[o.slot], 16)
                elif o.sig:
                    ins.then_inc(esem[ename], 1)
            if ename == "sp":
                for k, last in enumerate(self.ring_last):
                    if last is not None:
                        eng.wait_ge(dsem[k], last.ticket)

        @block.tensor
        def _(e):
            run("pe", e)

        @block.scalar
        def _(e):
            run("act", e)

        @block.vector
        def _(e):
            run("dve", e)

        @block.gpsimd
        def _(e):
            run("pool", e)

        @block.sync
        def _(e):
            run("sp", e)


def build_program(NT):
    NSLOT = NT // 8
    nc = bass.Bass("TRN2", target_bir_lowering=False)
    P = Prog()

    def din(name, shape, dt=F32):
        return nc.dram_tensor(name, list(shape), dt, kind="ExternalInput")

    def dout(name, shape, dt=F32):
        return nc.dram_tensor(name, list(shape), dt, kind="ExternalOutput")

    xp = din("xp", [NT * 128, D])
    kval_d = din("kval", [128, NT])
    dsel_d = din("dsel", [9, 128, 512], BF16)
    dprev_d = din("dprev", [128, 512], BF16)
    aprev_d = din("aprev", [128, 128])
    aown_d = din("aown", [128, 128])
    mc_d = din("mc", [128, 512])
    tri_d = din("tri", [128, 128])
    ones_d = din("ones", [128, 128])
    ident_d = din("ident", [128, 128], BF16)
    w_in0 = din("w_in0", [D, 2304])
    w_out0 = din("w_out0", [D, D])
    w_in1 = din("w_in1", [D, 4112])
    w_out1 = din("w_out1", [D, D])
    w_pool = din("w_pool", [4, 128, 128])
    g0_d = din("g0c", [128, 8])
    g1_d = din("g1c", [128, 8])
    vecs_d = din("vecs", [1, 1024])

    xs_d = din("xs_in", [2, 32, D])
    spool_d = din("spool", [2, 15, 512])
    cswk_d = din("cswk", [2, 128, 128])
    cswv_d = din("cswv", [2, 128, 128])
    cfk_d = din("cfk", [2, 4096, D])
    cfv_d = din("cfv", [2, 4096, D])
    cfl_d = din("cfl", [2, 4096, 16])
    ys_o = dout("ys_o", [2, 32, D])
    pools_o = dout("pools_o", [2, 15, 512])
    swks_o = dout("swks_o", [2, 128, 128])
    swvs_o = dout("swvs_o", [2, 128, 128])
    fks_o = dout("fks_o", [2, 32, D])
    fvs_o = dout("fvs_o", [2, 32, D])
    fls_o = dout("fls_o", [2, 32, 16])

    y_o = dout("y_o", [NSLOT, 128, D])
    fk_o = dout("fk_o", [NSLOT, 128, D])
    fv_o = dout("fv_o", [NSLOT, 128, D])
    fl_o = dout("fl_o", [NSLOT, 128, 16])
    pool_o = dout("pool_o", [15, 512])
    swk_o = dout("swk_o", [128, 128])
    swv_o = dout("swv_o", [128, 128])

    KTP = 96
    KT_d = nc.dram_tensor("KT_d", [NT, KTP, 2048], BF16)
    V_d = nc.dram_tensor("V_d", [NT, 128, 1040], BF16)
    W_d = nc.dram_tensor("W_d", [3, 128, 8192], BF16)

    es = ExitStack()
    with es:
        def sb(name, shape, dt=F32):
            return es.enter_context(nc.sbuf_tensor(name, list(shape), dt))

        W0in = sb("W0in", [128, 8 * 2304], BF16)
        W0out = sb("W0out", [128, 8 * 1024], BF16)
        W1kv = sb("W1kv", [128, 8 * 2064], BF16)
        Wpool = sb("Wpool", [128, 512], BF16)
        Wst = sb("Wst", [128, 8192], BF16)
        Aprev = sb("Aprev", [128, 128]); Aown = sb("Aown", [128, 128]); Mc = sb("Mc", [128, 512])
        Tri = sb("Tri", [128, 128]); Ones = sb("Ones", [128, 128]); Ident = sb("Ident", [128, 128], BF16)
        Dprev = sb("Dprev", [128, 512], BF16); Dcur = sb("Dcur", [128, 512], BF16)
        Kval = sb("Kval", [128, NT])
        G0c = sb("G0c", [128, 8]); G1c = sb("G1c", [128, 8])
        Vecs = sb("Vecs", [128, 800])
        Esink = sb("Esink", [128, 8])
        EPSC = sb("EPSC", [128, 1]); ONEC = sb("ONEC", [128, 1])
        xin = [sb("xin0", [128, D]), sb("xin1", [128, D])]
        tmpA = sb("tmpA", [128, D]); tmpB = sb("tmpB", [128, D])
        MIXb = [sb("MIX0", [128, D]), sb("MIX1", [128, D])]
        MIX = MIXb[0]
        tmpE = sb("tmpE", [128, D])
        x1b = [sb("x1_0", [128, D]), sb("x1_1", [128, D])]
        x1 = x1b[0]
        tmpC = sb("tmpC", [128, 1040]); tmpD = sb("tmpD", [128, D])
        xs = sb("xs", [128, D], BF16)
        xsB = sb("xsB", [128, D], BF16)
        T1 = sb("T1", [128, D], BF16); T2 = sb("T2", [128, D], BF16)
        QT = sb("QT", [128, 2048], BF16)
        PT = [sb("PT0", [128, D], BF16), sb("PT1", [128, D], BF16)]
        Gb = sb("Gb", [128, D], BF16)
        KTo = sb("KTo", [128, 2048], BF16)
        Qaug = sb("Qaug", [128, 16 * KA], BF16)
        Kaug = sb("Kaug", [128, 16 * KA], BF16)
        Vaug = sb("Vaug", [128, 1040], BF16)
        q0aug2 = [sb("q0aug", [128, 8 * 65], BF16), sb("q0augB", [128, 8 * 65], BF16)]
        k0aug2 = [sb("k0aug", [128, 2 * 65], BF16), sb("k0augB", [128, 2 * 65], BF16)]
        q0aug, k0aug = q0aug2[0], k0aug2[0]
        v0aug = [sb("v0aug%d" % i, [128, 130], BF16) for i in range(3)]
        KT0 = [sb("KT0a", [128, 256], BF16), sb("KT0b", [128, 256], BF16)]
        ubf = [sb("ubf%d" % i, [128, 512], BF16) for i in range(3)]
        diffT = sb("diffT", [128, 512], BF16)
        KTr = [sb("KTr%d" % i, [128, 2048], BF16) for i in range(2)]
        Vr = [sb("Vr%d" % i, [128, 1040], BF16) for i in range(2)]
        Oe = tmpC
        sm = sb("sm", [128, 256])
        cum = sb("cum", [128, 16]); carry = sb("carry", [128, 16]); logf = sb("logf", [128, 16]); logfn = sb("logfn", [128, 16])
        c8 = sb("c8", [128, 64]); chi = sb("chi", [128, 48], BF16)
        psb = [es.enter_context(nc.psum_tensor("ps%d" % i, [128, 512], F32)) for i in range(8)]
        esem = {e: es.enter_context(nc.semaphore("s_" + e)) for e in Prog.ENG}
        dsem = [es.enter_context(nc.semaphore("d%d" % i)) for i in range(Prog.NRING)]
        block = es.enter_context(nc.Block())

        B = {}

        def bf(name):
            if name not in B:
                B[name] = Buf(name)
            return B[name]

        bank = [bf("bank%d" % i) for i in range(8)]
        PTB = [(bf("PTq0"), bf("PTq1")), (bf("PTq2"), bf("PTq3"))]
        for b_ in bank:
            b_.excl = True

        def PB(b, lo=0, hi=512):
            return psb[b][:, lo:hi]

        def PBb(b):
            return psb[b][:, :].bitcast(BF16)

        def dma(out, in_, r, w):
            return P.op("sp", lambda e: e.dma_start(out=out, in_=in_), r, w, dma=True)

        def mm(out, lhsT, rhs, start, stop, r, w):
            return P.op("pe", lambda e: e.matmul(out, lhsT, rhs, start=start, stop=stop,
                                                 skip_group_check=True), r, w)

        def tr(out, in_, r, w):
            n = in_.shape[0]
            return P.op("pe", lambda e: e.transpose(out, in_, Ident[0:n, 0:n]), r, w)

        def act(out, in_, func, r, w, scale=1.0, bias=0.0, accum=None):
            if accum is None:
                return P.op("act", lambda e: e.activation(out, in_, func, bias=bias, scale=scale), r, w)
            return P.op("act", lambda e: e.activation(out, in_, func, bias=bias, scale=scale,
                                                      accum_out=accum), r, w)

        def ts(eng, out, in0, s1, s2, op0, op1, r, w):
            if op1 is None:
                return P.op(eng, lambda e: e.tensor_scalar(out, in0, s1, None, op0), r, w)
            return P.op(eng, lambda e: e.tensor_scalar(out, in0, s1, s2, op0, op1), r, w)

        def tt(eng, out, in0, in1, op, r, w):
            return P.op(eng, lambda e: e.tensor_tensor(out, in0, in1, op), r, w)

        def stt(eng, out, in0, scalar, in1, op0, op1, r, w):
            return P.op(eng, lambda e: e.scalar_tensor_tensor(out, in0, scalar, in1, op0, op1), r, w)

        def cp(eng, out, in_, r, w):
            if eng == "act":
                return P.op("act", lambda e: e.copy(out, in_), r, w)
            return P.op(eng, lambda e: e.tensor_copy(out, in_), r, w)

        def memset(eng, ap, val, w):
            return P.op(eng, lambda e: e.memset(ap, val), (), w)

        def red(eng, out, in_, r, w):
            return P.op(eng, lambda e: e.tensor_reduce(out, in_, AX.X, ALU.add), r, w)

        def recip(out, in_, r, w):
            return P.op("dve", lambda e: e.reciprocal(out, in_), r, w)

        def v3(ap, h):
            return ap.rearrange("p (h d) -> p h d", h=h)

        bT = bf("tables")
        for t_sb, t_d in ((Aprev, aprev_d), (Aown, aown_d), (Mc, mc_d), (Tri, tri_d), (Ones, ones_d),
                          (Ident, ident_d), (Dprev, dprev_d), (Kval, kval_d), (G0c, g0_d), (G1c, g1_d)):
            dma(t_sb[:, :], t_d[:, :], (), (bT,))
        dma(Vecs[:, :], vecs_d[0:1, 0:800].partition_broadcast(128), (), (bT,))
        PSC = Vecs[:, 0:512]
        QN0 = Vecs[:, 512:576]; KN0 = Vecs[:, 576:640]
        BFG = Vecs[:, 656:672]
        QN1 = Vecs[:, 672:736]; KN1 = Vecs[:, 736:800]
        memset("dve", EPSC[:, :], EPS, (bT,))
        memset("dve", ONEC[:, :], 1.0, (bT,))
        act(Esink[:, :], Vecs[:, 640:648], AF.Exp, (bT,), (bf("esink"),))

        bq0, bk0, bQ, bK, bV = bf("q0aug"), bf("k0aug"), bf("Qaug"), bf("Kaug"), bf("Vaug")
        memset("pool", q0aug2[0][:, :], 1.0, (bq0,))
        memset("pool", q0aug2[1][:, :], 1.0, (bf("q0augB"),))
        memset("pool", v0aug[0][:, :], 1.0, (bf("v0aug0"),))
        memset("pool", v0aug[1][:, :], 1.0, (bf("v0aug1"),))
        memset("pool", v0aug[2][:, :], 0.0, (bf("v0aug2"),))
        memset("pool", ubf[2][:, :], 0.0, (bf("ubf2"),))
        memset("pool", KT0[1][:, :], 0.0, (bf("KT0b"),))
        memset("pool", KT0[1][64:65, :], NEG, (bf("KT0b"),))
        memset("pool", Qaug[:, :], 1.0, (bQ,))
        memset("pool", KTo[:, :], 0.0, (bf("KTo"),))
        memset("pool", Kaug[:, :], 1.0, (bK,))
        memset("pool", Vaug[:, :], 1.0, (bV,))
        memset("pool", carry[:, :], 0.0, (bf("carry"),))
        memset("dve", v0aug[2][:, 64:65], 1.0, (bf("v0aug2"),))
        memset("dve", v0aug[2][:, 129:130], 1.0, (bf("v0aug2"),))

        bA, bBt = bf("tmpA"), bf("tmpB")
        stage = [(tmpA, bA), (tmpB, bBt)]
        cnt = [0]

        def load_w(src, ncols_total, c0, c1, dst, dst_stride, dst_off, gcol, dst_buf):
            for kc in range(8):
                for a in range(c0, c1, 1024):
                    b_ = min(a + 1024, c1)
                    st, sbuf = stage[cnt[0] % 2]
                    eng = "dve"
                    cnt[0] += 1
                    dma(st[:, 0:b_ - a], src[kc * 128:(kc + 1) * 128, a:b_], (), (sbuf,))
                    o = dst[:, kc * dst_stride + dst_off + (a - c0): kc * dst_stride + dst_off + (b_ - c0)]
                    if gcol is None:
                        cp(eng, o, st[:, 0:b_ - a], (sbuf,), (dst_buf,))
                    else:
                        ts(eng, o, st[:, 0:b_ - a], gcol[:, kc:kc + 1], None, ALU.mult, None,
                           (sbuf, bT), (dst_buf,))

        bW = bf("weights")
        bWst = bf("Wst")
        load_w(w_in0, 2304, 0, 2304, W0in, 2304, 0, G0c, bW)
        load_w(w_out0, 1024, 0, 1024, W0out, 1024, 0, None, bW)
        load_w(w_in1, 4112, 1024, 3072, W1kv, 2064, 0, G1c, bW)
        load_w(w_in1, 4112, 4096, 4112, W1kv, 2064, 2048, G1c, bW)
        for g in range(4):
            st, sbuf = stage[g % 2]
            dma(st[:, 0:128], w_pool[g, :, :], (), (sbuf,))
            cp("dve", Wpool[:, g * 128:(g + 1) * 128], st[:, 0:128], (sbuf,), (bW,))
        bWd = bf("W_d")
        for i, (src, c0, gcol) in enumerate(((w_in1, 0, G1c), (w_in1, 3072, G1c), (w_out1, 0, None))):
            load_w(src, 0, c0, c0 + 1024, Wst, 1024, 0, gcol, bWst)
            dma(W_d[i, :, :], Wst[:, :], (bWst,), (bWd,))

        bsm = bf("sm")

        def rms_to_T(src_ap, src_buf, dstT, dstT_buf, n=128, xs_t=None, xs_n="xs", smc=0, bt=0):
            xs_t = xs if xs_t is None else xs_t
            bxs, bsr = bf(xs_n), bf("sm_rms%d" % smc)
            act(xs_t[0:n, :], src_ap, AF.Square, (src_buf,), (bxs, bsr), accum=sm[0:n, smc:smc + 1])
            act(sm[0:n, smc + 1:smc + 2], sm[0:n, smc:smc + 1], AF.Ln, (bsr, bT), (bsr,), scale=1.0 / D, bias=EPSC[0:n, :])
            act(sm[0:n, smc + 1:smc + 2], sm[0:n, smc + 1:smc + 2], AF.Exp, (bsr,), (bsr,), scale=-0.5)
            ts("dve", xs_t[0:n, :], src_ap, sm[0:n, smc + 1:smc + 2], None, ALU.mult, None, (src_buf, bsr), (bxs,))
            for kc in range(8):
                tr(PBb(bt)[:, kc * 128: kc * 128 + n], xs_t[0:n, kc * 128:(kc + 1) * 128], (bxs, bT), (bank[bt],))
            if n == 128:
                cp("act", dstT[:, :], PBb(bt)[:, :], (bank[bt],), (dstT_buf,))
            else:
                cp("act", v3(dstT[:, :], 8)[:, :, 0:n], v3(PBb(bt)[:, :], 8)[:, :, 0:n], (bank[bt],), (dstT_buf,))

        FILL = [0]

        def filler():
            for _ in range(FILL[0]):
                mm(PB(2)[:, :], Ident[:, :], W0out[:, 0:512], True, True, (bT, bW), (bank[2],))

        def proj(hT, hT_buf, W, wstride, c0, ncols, b, n=128, wbuf=None):
            for kc in range(8):
                mm(PB(b, 0, ncols)[0:n, :], hT[:, kc * 128: kc * 128 + n], W[:, kc * wstride + c0: kc * wstride + c0 + ncols],
                   kc == 0, kc == 7, (hT_buf, wbuf or bW), (bank[b],))
            filler()

        def head_norm(src_ap, src_bufs, nh, gain, out_f32, out_buf, smcol, n=128, scr=None, scr_b=None):
            scr = tmpB if scr is None else scr
            scr_b = bBt if scr_b is None else scr_b
            bs = bf("sm_hn%d" % smcol)
            smv = sm[0:n, smcol:smcol + nh]
            act(scr[0:n, 0:nh * 64], src_ap, AF.Square, src_bufs, (scr_b,))
            red("dve", smv, v3(scr[0:n, 0:nh * 64], nh), (scr_b,), (bs,))
            act(smv, smv, AF.Ln, (bs, bT), (bs,), scale=1.0 / 64, bias=EPSC[0:n, :])
            act(smv, smv, AF.Exp, (bs,), (bs,), scale=-0.5)
            tt("dve", v3(out_f32, nh), v3(src_ap, nh), smv.unsqueeze(2).to_broadcast([n, nh, 64]), ALU.mult,
               tuple(src_bufs) + (bs,), (out_buf,))
            tt("dve", v3(out_f32, nh), v3(out_f32, nh), gain[0:n, :].unsqueeze(1).to_broadcast([n, nh, 64]), ALU.mult,
               (out_buf, bT), (out_buf,))

        def silu_gate(gate_banks, out_ap, out_buf, n=128):
            for i, b in enumerate(gate_banks):
                sl = slice(i * 512, (i + 1) * 512)
                act(out_ap[0:n, sl], PB(b)[0:n, :], AF.Exp, (bank[b],), (out_buf,), scale=-1.0)
            act(out_ap[0:n, :], out_ap[0:n, :], AF.Ln, (out_buf, bT), (out_buf,), bias=ONEC[0:n, :])
            act(out_ap[0:n, :], out_ap[0:n, :], AF.Exp, (out_buf,), (out_buf,), scale=-1.0)
            for i, b in enumerate(gate_banks):
                sl = slice(i * 512, (i + 1) * 512)
                tt("dve", out_ap[0:n, sl], out_ap[0:n, sl], PB(b)[0:n, :], ALU.mult, (out_buf, bank[b]), (out_buf,))

        def out_proj(G_ap, G_buf, W, wbuf, res_ap, res_buf, dst_ap, dst_buf, n=128, bt=0, bo=(2, 3), T1=T1, t1n="T1"):
            bf_T1 = bf(t1n)
            cp("act", Gb[0:n, :], G_ap, (G_buf,), (bf("Gb"),))
            for kc in range(8):
                tr(PBb(bt)[:, kc * 128: kc * 128 + n], Gb[0:n, kc * 128:(kc + 1) * 128], (bf("Gb"), bT), (bank[bt],))
            if n == 128:
                cp("act", T1[:, :], PBb(bt)[:, :], (bank[bt],), (bf_T1,))
            else:
                cp("act", v3(T1[:, :], 8)[:, :, 0:n], v3(PBb(bt)[:, :], 8)[:, :, 0:n], (bank[bt],), (bf_T1,))
            for half in range(2):
                b = bo[half]
                for kc in range(8):
                    mm(PB(b)[0:n, :], T1[:, kc * 128: kc * 128 + n], W[:, kc * 1024 + half * 512: kc * 1024 + (half + 1) * 512],
                       kc == 0, kc == 7, (bf_T1, wbuf), (bank[b],))
                filler()
                tt("dve", dst_ap[0:n, half * 512:(half + 1) * 512], PB(b)[0:n, :], res_ap[0:n, half * 512:(half + 1) * 512],
                   ALU.add, (bank[b], res_buf), (dst_buf,))

        slopes = [2.0 ** (-(h + 1)) for h in range(8)]

        bMIX = bf("MIX0")
        bC, bDt = bf("tmpC"), bf("tmpD")

        def L0a(v):
            bxin = bf("xin%d" % (v % 2))
            X = xin[v % 2]
            q0a, bq = q0aug2[v % 2], bf("q0aug" if v % 2 == 0 else "q0augB")
            k0a, bk = k0aug2[v % 2], bf("k0aug" if v % 2 == 0 else "k0augB")
            u3, bu = ubf[v % 3], bf("ubf%d" % (v % 3))
            v3a, bv0 = v0aug[v % 3], bf("v0aug%d" % (v % 3))
            Mx, bMx = MIXb[v % 2], bf("MIX%d" % (v % 2))
            dma(X[:, :], xp[v * 128:(v + 1) * 128, :], (), (bxin,))
            rms_to_T(X[:, :], bxin, T1, bf("T1"), bt=0)
            proj(T1, bf("T1"), W0in, 2304, 0, 512, 0)
            proj(T1, bf("T1"), W0in, 2304, 512, 512, 1)
            cp("act", u3[:, :], PB(0)[:, :], (bank[0],), (bu,))
            if v == NT - 1:
                cp("dve", tmpA[:, 0:512], PB(0)[:, :], (bank[0],), (bA,))
                dma(pool_o[:, :], tmpA[113:128, 0:512], (bA,), (bf("pool_o"),))
            proj(T1, bf("T1"), W0in, 2304, 1024, 256, 0)
            head_norm(PB(1)[:, :], (bank[1],), 8, QN0, tmpA[:, 0:512], bA, 8)
            cp("dve", v3(q0a[:, :], 8)[:, :, 0:64], v3(tmpA[:, 0:512], 8), (bA,), (bq,))
            proj(T1, bf("T1"), W0in, 2304, 1280, 512, 1)
            head_norm(PB(0, 0, 128)[:, :], (bank[0],), 2, KN0, tmpA[:, 512:640], bA, 16)
            cp("act", v3(k0a[:, :], 2)[:, :, 0:64], v3(tmpA[:, 512:640], 2), (bA,), (bk,))
            ts("dve", v3(k0a[:, :], 2)[:, :, 64:65], Ones[:, 0:2].unsqueeze(2), Kval[:, v:v + 1], None, ALU.mult, None, (bT,), (bk,))
            cp("act", v3(v3a[:, :], 2)[:, :, 0:64], v3(PB(0, 128, 256)[:, :], 2), (bank[0],), (bv0,))
            if v == NT - 1:
                dma(swk_o[:, :], tmpA[:, 512:640], (bA,), (bf("swk_o"),))
                cp("dve", tmpA[:, 640:768], PB(0, 128, 256)[:, :], (bank[0],), (bA,))
                dma(swv_o[:, :], tmpA[:, 640:768], (bA,), (bf("swv_o"),))
            proj(T1, bf("T1"), W0in, 2304, 1792, 512, 0)
            silu_gate((1, 0), Mx, bMx)

        def L0b(v):
            cur, prv = v % 2, (v + 1) % 2
            bxin = bf("xin%d" % cur)
            X = xin[cur]
            bx1 = bf("x1_%d" % cur)
            q0a, bq = q0aug2[cur], bf("q0aug" if cur == 0 else "q0augB")
            k0a, bk = k0aug2[cur], bf("k0aug" if cur == 0 else "k0augB")
            uc, buc = ubf[v % 3], bf("ubf%d" % (v % 3))
            up, bup = ubf[(v - 1) % 3], bf("ubf%d" % ((v - 1) % 3))
            vc, bvc = v0aug[v % 3], bf("v0aug%d" % (v % 3))
            vp, bvp = v0aug[(v - 1) % 3], bf("v0aug%d" % ((v - 1) % 3))
            Mx, bMx = MIXb[cur], bf("MIX%d" % cur)
            bE = bf("tmpE")
            T3 = QT[:, 1024:2048]
            if v < 9:
                dma(Dcur[:, :], dsel_d[v, :, :], (), (bf("Dcur"),))
            for g in range(4):
                mm(PB(3, g * 128, (g + 1) * 128)[:, :], uc[:, g * 128:(g + 1) * 128], Dcur[:, g * 128:(g + 1) * 128],
                   g == 0, False, (buc, bf("Dcur")), (bank[3],))
                mm(PB(3, g * 128, (g + 1) * 128)[:, :], up[:, g * 128:(g + 1) * 128], Dprev[:, g * 128:(g + 1) * 128],
                   False, True, (bup, bT), (bank[3],))
            cp("act", diffT[:, :], PB(3)[:, :], (bank[3],), (bf("diffT"),))
            for g in range(4):
                mm(PB(4, g * 128, (g + 1) * 128)[:, :], diffT[:, g * 128:(g + 1) * 128], Wpool[:, g * 128:(g + 1) * 128],
                   g == 0, g == 3, (bf("diffT"), bW), (bank[4],))
            for h in range(8):
                tr(PBb(5)[0:65, h * 128:(h + 1) * 128], q0a[:, h * 65:(h + 1) * 65], (bq, bT), (bank[5],))
            cp("act", QT[0:65, 0:1024], PBb(5)[0:65, :], (bank[5],), (bf("QT"),))
            bkt = bf("KT0%s" % "ab"[cur])
            for g in range(2):
                tr(PBb(3)[0:65, g * 128:(g + 1) * 128], k0a[:, g * 65:(g + 1) * 65], (bk, bT), (bank[3],))
            cp("dve", KT0[cur][0:65, :], PBb(3)[0:65, 0:256], (bank[3],), (bkt,))
            tt("dve", tmpE[:, 0:512], PB(4)[:, :], PSC, ALU.mult, (bank[4], bT), (bE,))
            sbs = (5, 3)
            for kt_i, (ktb, ktbuf, A) in enumerate(((KT0[prv], bf("KT0%s" % "ab"[prv]), Aprev), (KT0[cur], bkt, Aown))):
                for g in range(2):
                    mm(PB(sbs[g])[:, :], ktb[0:65, g * 128:(g + 1) * 128], QT[0:65, g * 512:(g + 1) * 512],
                       True, True, (ktbuf, bf("QT")), (bank[sbs[g]],))
                for h in range(8):
                    b = sbs[h // 4]
                    o = PB(b, (h % 4) * 128, (h % 4 + 1) * 128)
                    stt("dve", o[:, :], A[:, :], -8.0 * slopes[h], o[:, :], ALU.mult, ALU.add, (bT, bank[b]), (bank[b],))
                for g in range(2):
                    act(PT[kt_i][:, g * 512:(g + 1) * 512], PB(sbs[g])[:, :], AF.Exp, (bank[sbs[g]],), (*PTB[kt_i],), scale=0.125)
            obs = (4, 5)
            for h in range(8):
                b = obs[h // 4]
                o = PB(b, (h % 4) * 65, (h % 4) * 65 + 65)
                mm(o[:, :], PT[0][:, h * 128:(h + 1) * 128], vp[:, (h // 4) * 65:(h // 4) * 65 + 65],
                   h % 4 == 0, False, (*PTB[0], bvp), (bank[b],))
                mm(o[:, :], PT[1][:, h * 128:(h + 1) * 128], vc[:, (h // 4) * 65:(h // 4) * 65 + 65],
                   False, True, (*PTB[1], bvc), (bank[b],))
            for half in range(2):
                b = obs[half]
                O3 = PB(b, 0, 260).rearrange("p (h d) -> p h d", h=4)
                tt("dve", sm[:, 32 + half * 4: 36 + half * 4].unsqueeze(2), O3[:, :, 64:65],
                   Esink[:, half * 4:(half + 1) * 4].unsqueeze(2), ALU.add, (bank[b], bf("esink")), (bf("sm_l0"),))
                recip(sm[:, 32 + half * 4: 36 + half * 4], sm[:, 32 + half * 4: 36 + half * 4], (bf("sm_l0"),), (bf("sm_l0"),))
                tt("dve", v3(tmpE[:, 512 + half * 256: 768 + half * 256], 4), O3[:, :, 0:64],
                   sm[:, 32 + half * 4: 36 + half * 4].unsqueeze(2).to_broadcast([128, 4, 64]), ALU.mult,
                   (bank[b], bf("sm_l0")), (bE,))
            tt("dve", Mx[:, :], Mx[:, :], tmpE[:, :], ALU.mult, (bMx, bE), (bMx,))
            out_proj(Mx[:, :], bMx, W0out, bW, X, bxin, x1b[cur], bx1, bt=3, bo=(4, 5), T1=T3, t1n="QT")

        def L1a(v):
            owned = (v % 8 == 7)
            slot = v // 8
            xin1, bx1 = x1b[v % 2], bf("x1_%d" % (v % 2))
            rms_to_T(xin1[:, :], bx1, T2, bf("T2"), xs_t=xsB, xs_n="xsB", smc=128, bt=6)
            proj(T2, bf("T2"), W1kv, 2064, 0, 512, 6)
            proj(T2, bf("T2"), W1kv, 2064, 512, 512, 7)
            blg = bf("logf")
            head_norm(PB(6)[:, :], (bank[6],), 8, KN1, tmpC[:, 0:512], bC, 136, scr=tmpD, scr_b=bDt)
            proj(T2, bf("T2"), W1kv, 2064, 2048, 16, 6)
            head_norm(PB(7)[:, :], (bank[7],), 8, KN1, tmpC[:, 512:1024], bC, 144, scr=tmpD, scr_b=bDt)
            tt("dve", logf[:, :], PB(6, 0, 16)[:, :], BFG, ALU.add, (bank[6], bT), (blg,))
            act(logf[:, :], logf[:, :], AF.Exp, (blg,), (blg,), scale=-1.0)
            act(logf[:, :], logf[:, :], AF.Ln, (blg, bT), (blg,), bias=ONEC[:, :])
            ts("dve", logf[:, :], logf[:, :], -1.0, None, ALU.mult, None, (blg,), (blg,))
            proj(T2, bf("T2"), W1kv, 2064, 1536, 512, 7)
            mm(PB(6, 16, 32)[:, :], Tri[:, :], logf[:, :], True, True, (bT, blg), (bank[6],))
            mm(PB(6, 32, 48)[:, :], Ones[:, :], logf[:, :], True, True, (bT, blg), (bank[6],))
            tt("dve", cum[:, :], PB(6, 16, 32)[:, :], carry[:, :], ALU.add, (bank[6], bf("carry")), (bf("cum"),))
            tt("dve", carry[:, :], PB(6, 32, 48)[:, :], carry[:, :], ALU.add, (bank[6], bf("carry")), (bf("carry"),))
            proj(T2, bf("T2"), W1kv, 2064, 1024, 512, 6)
            bc8 = bf("c8")
            ts("dve", c8[:, 0:16], cum[:, :], 8.0, None, ALU.mult, None, (bf("cum"),), (bc8,))
            cp("dve", chi[:, 0:16], c8[:, 0:16], (bc8,), (bf("chi"),))
            tt("dve", c8[:, 16:32], c8[:, 0:16], chi[:, 0:16], ALU.subtract, (bc8, bf("chi")), (bc8,))
            cp("dve", chi[:, 16:32], c8[:, 16:32], (bc8,), (bf("chi"),))
            tt("dve", c8[:, 32:48], c8[:, 16:32], chi[:, 16:32], ALU.subtract, (bc8, bf("chi")), (bc8,))
            cp("dve", chi[:, 32:48], c8[:, 32:48], (bc8,), (bf("chi"),))
            K3 = v3(Kaug[:, :], 16)
            cp("dve", K3[:, :, 0:64], v3(tmpC[:, 0:1024], 16), (bC,), (bK,))
            for i in range(3):
                ts("dve", K3[:, :, 67 + i:68 + i], chi[:, 16 * i:16 * i + 16].unsqueeze(2), -1.0, None, ALU.mult, None,
                   (bf("chi"),), (bK,))
            ts("dve", K3[:, :, 70:71], Ones[:, 0:16].unsqueeze(2), Kval[:, v:v + 1], None, ALU.mult, None, (bT,), (bK,))
            V3 = v3(Vaug[:, :], 16)
            cp("act", V3[:, 0:8, 0:64], v3(PB(6)[:, :], 8), (bank[6],), (bV,))
            cp("act", V3[:, 8:16, 0:64], v3(PB(7)[:, :], 8), (bank[7],), (bV,))
            if owned:
                dma(fk_o[slot, :, :], tmpC[:, 0:1024], (bC,), (bf("fk_o"),))
                cp("dve", tmpD[:, 0:512], PB(6)[:, :], (bank[6],), (bDt,))
                cp("dve", tmpD[:, 512:1024], PB(7)[:, :], (bank[7],), (bDt,))
                dma(fv_o[slot, :, :], tmpD[:, :], (bDt,), (bf("fv_o"),))
                dma(fl_o[slot, :, :], logf[:, :], (blg,), (bf("fl_o"),))
            for h in range(16):
                b = 6 + h // 8
                tr(PBb(b)[0:KA, (h % 8) * 128:(h % 8 + 1) * 128], Kaug[:, h * KA:(h + 1) * KA], (bK, bT), (bank[b],))
            cp("act", KTo[0:KA, 0:1024], PBb(6)[0:KA, :], (bank[6],), (bf("KTo"),))
            cp("dve", KTo[0:KA, 1024:2048], PBb(7)[0:KA, :], (bank[7],), (bf("KTo"),))
            bKTd, bVd = bf("KT_d%d" % v), bf("V_d%d" % v)
            dma(KT_d[v, :, :], KTo[0:KTP, :], (bf("KTo"),), (bKTd,))
            dma(V_d[v, :, :], Vaug[:, :], (bV,), (bVd,))

        L0a(0)
        L0b(0)
        if NT > 1:
            L0a(1)
        for v in range(NT):
            owned = (v % 8 == 7)
            slot = v // 8
            FILL[0] = 1
            sA = P.captured(lambda: L0a(v + 2)) if v + 2 < NT else []
            sB = P.captured(lambda: L0b(v + 1)) if v + 1 < NT else []
            sC = P.captured(lambda: L1a(v))
            FILL[0] = 0
            for sp in Prog.interleave(Prog.interleave(sA, sB), sC):
                P.commit(*sp)
            x1 = x1b[v % 2]
            bx1 = bf("x1_%d" % (v % 2))
            if not owned:
                continue
            dma(Wst[:, :], W_d[0, :, :], (bWd,), (bWst,))
            proj(T2, bf("T2"), Wst, 1024, 0, 512, 1, wbuf=bWst)
            proj(T2, bf("T2"), Wst, 1024, 512, 512, 2, wbuf=bWst)
            head_norm(PB(1)[:, :], (bank[1],), 8, QN1, tmpA[:, 0:512], bA, 8)
            head_norm(PB(2)[:, :], (bank[2],), 8, QN1, tmpA[:, 512:1024], bA, 16)
            Q3 = v3(Qaug[:, :], 16)
            cp("act", Q3[:, :, 0:64], v3(tmpA[:, :], 16), (bA,), (bQ,))
            for i in range(3):
                cp("dve", Q3[:, :, 64 + i:65 + i], chi[:, 16 * i:16 * i + 16].unsqueeze(2), (bf("chi"),), (bQ,))
            for h in range(16):
                b = 6 + h // 8
                tr(PBb(b)[0:KA, (h % 8) * 128:(h % 8 + 1) * 128], Qaug[:, h * KA:(h + 1) * KA], (bQ, bT), (bank[b],))
            cp("act", QT[0:KA, 0:1024], PBb(6)[0:KA, :], (bank[6],), (bf("QT"),))
            cp("dve", QT[0:KA, 1024:2048], PBb(7)[0:KA, :], (bank[7],), (bf("QT"),))
            dma(Wst[:, :], W_d[1, :, :], (bWd,), (bWst,))
            proj(T2, bf("T2"), Wst, 1024, 0, 512, 1, wbuf=bWst)
            proj(T2, bf("T2"), Wst, 1024, 512, 512, 2, wbuf=bWst)
            silu_gate((1, 2), tmpE, bf("tmpE"))
            dma(Wst[:, :], W_d[2, :, :], (bWd,), (bWst,))
            if KSTOP <= 6:
                continue
            def obank(h):
                return (6, 7, 1)[h // 7], (h % 7) * 65
            LAG = 2
            NR = len(KTr)
            units = [(kt, qd) for kt in range(v + 1) for qd in range(4)]

            def emit_pv(u):
                kt, qd = units[u]
                r_ = kt % NR
                pq = u % 4
                ptb = PT[pq // 2][:, (pq % 2) * 512:(pq % 2 + 1) * 512]
                for hh in range(4):
                    h = qd * 4 + hh
                    ob, oc = obank(h)
                    mm(PB(ob, oc, oc + 65)[:, :], ptb[:, hh * 128:(hh + 1) * 128], Vr[r_][:, h * 65:(h + 1) * 65],
                       kt == 0 and h in (0, 7, 14), kt == v, (bf("PTq%d" % pq), bf("Vr%d" % r_)), (bank[ob],))

            for u, (kt, qd) in enumerate(units):
                r_ = kt % NR
                bktr, bvr = bf("KTr%d" % r_), bf("Vr%d" % r_)
                if qd == 0:
                    dma(KTr[r_][0:KTP, :], KT_d[kt, :, :], (bf("KT_d%d" % kt),), (bktr,))
                    dma(Vr[r_][:, :], V_d[kt, :, :], (bf("V_d%d" % kt),), (bvr,))
                b = 2 + u % 4
                pq = u % 4
                for hh in range(4):
                    h = qd * 4 + hh
                    mm(PB(b, hh * 128, (hh + 1) * 128)[:, :], KTr[r_][0:KA, h * 128:(h + 1) * 128],
                       QT[0:KA, h * 128:(h + 1) * 128], hh == 0, hh == 3, (bktr, bf("QT")), (bank[b],))
                if kt == v:
                    tt("dve", PB(b)[:, :], PB(b)[:, :], Mc[:, :], ALU.add, (bank[b], bT), (bank[b],))
                act(PT[pq // 2][:, (pq % 2) * 512:(pq % 2 + 1) * 512], PB(b)[:, :], AF.Exp, (bank[b],), (bf("PTq%d" % pq),), scale=0.125)
                if u >= LAG:
                    emit_pv(u - LAG)
                mm(PB(0)[:, :], Ident[:, :], W0out[:, 0:512], True, True, (bT, bW), (bank[0],))
            for u in range(max(0, len(units) - LAG), len(units)):
                emit_pv(u)
            bOe = bf("tmpC")
            for gi, (ob, nh) in enumerate(((6, 7), (7, 7), (1, 2))):
                cp("act" if gi != 1 else "dve", Oe[:, gi * 455: gi * 455 + nh * 65], PB(ob, 0, nh * 65)[:, :], (bank[ob],), (bOe,))
            O3 = v3(Oe[:, :], 16)
            recip(sm[:, 64:80].unsqueeze(2), O3[:, :, 64:65], (bOe,), (bf("sm_l1"),))
            tt("dve", v3(tmpA[:, :], 16), O3[:, :, 0:64], sm[:, 64:80].unsqueeze(2).to_broadcast([128, 16, 64]), ALU.mult,
               (bOe, bf("sm_l1")), (bA,))
            tt("dve", tmpE[:, :], tmpE[:, :], tmpA[:, :], ALU.mult, (bf("tmpE"), bA), (bf("tmpE"),))
            out_proj(tmpE[:, :], bf("tmpE"), Wst, bWst, x1, bx1, tmpB, bBt)
            dma(y_o[slot, :, :], tmpB[:, :], (bBt,), (bf("y_o"),))

        NQ = 32
        PAST = 4096
        x1, bx1 = x1b[0], bf("x1_0")
        for sbi in range(2):
            bx = bf("xin%d" % (sbi % 2))
            X = xin[sbi % 2]
            dma(Dcur[:, :], dsel_d[8, :, :], (), (bf("Dcur"),))
            dma(X[0:NQ, :], xs_d[sbi, :, :], (), (bx,))
            rms_to_T(X[0:NQ, :], bx, T1, bf("T1"), n=NQ)
            proj(T1, bf("T1"), W0in, 2304, 0, 512, 1, n=NQ)
            proj(T1, bf("T1"), W0in, 2304, 512, 512, 2, n=NQ)
            proj(T1, bf("T1"), W0in, 2304, 1024, 256, 3, n=NQ)
            proj(T1, bf("T1"), W0in, 2304, 1280, 512, 4, n=NQ)
            proj(T1, bf("T1"), W0in, 2304, 1792, 512, 5, n=NQ)
            memset("pool", tmpA[:, 0:512], 0.0, (bA,))
            dma(tmpA[113:128, 0:512], spool_d[sbi, :, :], (), (bA,))
            cp("dve", ubf[1][:, :], tmpA[:, 0:512], (bA,), (bf("ubf1"),))
            cp("act", ubf[0][0:NQ, :], PB(1)[0:NQ, :], (bank[1],), (bf("ubf0"),))
            cp("dve", tmpB[0:NQ, 0:512], PB(1)[0:NQ, :], (bank[1],), (bBt,))
            dma(pools_o[sbi, :, :], tmpB[17:32, 0:512], (bBt,), (bf("pools_o"),))
            dma(tmpB[:, 512:640], cswk_d[sbi, :, :], (), (bBt,))
            dma(tmpB[:, 640:768], cswv_d[sbi, :, :], (), (bBt,))
            cp("act", v3(k0aug[:, :], 2)[:, :, 0:64], v3(tmpB[:, 512:640], 2), (bBt,), (bk0,))
            ts("dve", v3(k0aug[:, :], 2)[:, :, 64:65], Ones[:, 0:2].unsqueeze(2), 0.0, None, ALU.mult, None, (bT,), (bk0,))
            cp("dve", v3(v0aug[0][:, :], 2)[:, :, 0:64], v3(tmpB[:, 640:768], 2), (bBt,), (bf("v0aug0"),))
            for g in range(2):
                tr(PBb(6)[0:65, g * 128:(g + 1) * 128], k0aug[:, g * 65:(g + 1) * 65], (bk0, bT), (bank[6],))
            cp("dve", KT0[0][0:65, :], PBb(6)[0:65, 0:256], (bank[6],), (bf("KT0a"),))
            dma(swks_o[sbi, 0:96, :], cswk_d[sbi, 32:128, :], (), (bf("swks_o"),))
            dma(swvs_o[sbi, 0:96, :], cswv_d[sbi, 32:128, :], (), (bf("swvs_o"),))
            head_norm(PB(2)[0:NQ, :], (bank[2],), 8, QN0, tmpA[0:NQ, 0:512], bA, 8, n=NQ)
            cp("act", v3(q0aug[0:NQ, :], 8)[:, :, 0:64], v3(tmpA[0:NQ, 0:512], 8), (bA,), (bq0,))
            head_norm(PB(3, 0, 128)[0:NQ, :], (bank[3],), 2, KN0, tmpA[0:NQ, 512:640], bA, 16, n=NQ)
            cp("dve", v3(k0aug[0:NQ, :], 2)[:, :, 0:64], v3(tmpA[0:NQ, 512:640], 2), (bA,), (bk0,))
            dma(swks_o[sbi, 96:128, :], tmpA[0:NQ, 512:640], (bA,), (bf("swks_o"),))
            cp("act", v3(v0aug[1][0:NQ, :], 2)[:, :, 0:64], v3(PB(3, 128, 256)[0:NQ, :], 2), (bank[3],), (bf("v0aug1"),))
            cp("dve", tmpA[0:NQ, 640:768], PB(3, 128, 256)[0:NQ, :], (bank[3],), (bA,))
            dma(swvs_o[sbi, 96:128, :], tmpA[0:NQ, 640:768], (bA,), (bf("swvs_o"),))
            silu_gate((4, 5), MIX, bMIX, n=NQ)
            for g in range(4):
                mm(PB(6, g * 128, g * 128 + NQ)[:, :], ubf[0][0:NQ, g * 128:(g + 1) * 128], Dcur[0:NQ, g * 128:g * 128 + NQ],
                   g == 0, False, (bf("ubf0"), bf("Dcur")), (bank[6],))
                mm(PB(6, g * 128, g * 128 + NQ)[:, :], ubf[1][:, g * 128:(g + 1) * 128], Dprev[:, g * 128:g * 128 + NQ],
                   False, True, (bf("ubf1"), bT), (bank[6],))
            cp("act", v3(diffT[:, :], 4)[:, :, 0:NQ], v3(PB(6)[:, :], 4)[:, :, 0:NQ], (bank[6],), (bf("diffT"),))
            for g in range(4):
                mm(PB(7, g * 128, (g + 1) * 128)[0:NQ, :], diffT[:, g * 128:g * 128 + NQ], Wpool[:, g * 128:(g + 1) * 128],
                   g == 0, g == 3, (bf("diffT"), bW), (bank[7],))
            for h in range(8):
                tr(PBb(0)[0:65, h * 128:h * 128 + NQ], q0aug[0:NQ, h * 65:(h + 1) * 65], (bq0, bT), (bank[0],))
            cp("act", v3(QT[0:65, 0:1024], 8)[:, :, 0:NQ], v3(PBb(0)[0:65, :], 8)[:, :, 0:NQ], (bank[0],), (bf("QT"),))
            for g in range(2):
                tr(PBb(6)[0:65, g * 128:g * 128 + NQ], k0aug[0:NQ, g * 65:(g + 1) * 65], (bk0, bT), (bank[6],))
            cp("dve", v3(KT0[1][0:65, :], 2)[:, :, 0:NQ], v3(PBb(6)[0:65, 0:256], 2)[:, :, 0:NQ], (bank[6],), (bf("KT0b"),))
            for h in range(8):
                g = h // 4
                mm(PB(1, h * NQ, (h + 1) * NQ)[:, :], KT0[0][0:65, g * 128:(g + 1) * 128], QT[0:65, h * 128:h * 128 + NQ],
                   h == 0, h == 7, (bf("KT0a"), bf("QT")), (bank[1],))
                mm(PB(2, h * NQ, (h + 1) * NQ)[0:NQ, :], KT0[1][0:65, g * 128:g * 128 + NQ], QT[0:65, h * 128:h * 128 + NQ],
                   h == 0, h == 7, (bf("KT0b"), bf("QT")), (bank[2],))
            for h in range(8):
                o = PB(1, h * NQ, (h + 1) * NQ)
                stt("dve", o[:, :], Aprev[:, 0:NQ], -8.0 * slopes[h], o[:, :], ALU.mult, ALU.add, (bT, bank[1]), (bank[1],))
                o = PB(2, h * NQ, (h + 1) * NQ)
                stt("dve", o[0:NQ, :], Aown[0:NQ, 0:NQ], -8.0 * slopes[h], o[0:NQ, :], ALU.mult, ALU.add, (bT, bank[2]), (bank[2],))
            act(PT[0][:, 0:256], PB(1, 0, 256)[:, :], AF.Exp, (bank[1],), (*PTB[0],), scale=0.125)
            act(PT[1][0:NQ, 0:256], PB(2, 0, 256)[0:NQ, :], AF.Exp, (bank[2],), (*PTB[1],), scale=0.125)
            tt("dve", tmpA[0:NQ, 0:512], PB(7)[0:NQ, :], PSC[0:NQ, :], ALU.mult, (bank[7], bT), (bA,))
            for h in range(8):
                b_ = 5 + h // 4
                o = PB(b_, (h % 4) * 65, (h % 4) * 65 + 65)
                g = h // 4
                mm(o[0:NQ, :], PT[0][:, h * NQ:(h + 1) * NQ], v0aug[0][:, g * 65:g * 65 + 65],
                   h % 4 == 0, False, (*PTB[0], bf("v0aug0")), (bank[b_],))
                mm(o[0:NQ, :], PT[1][0:NQ, h * NQ:(h + 1) * NQ], v0aug[1][0:NQ, g * 65:g * 65 + 65],
                   False, True, (*PTB[1], bf("v0aug1")), (bank[b_],))
            for half in range(2):
                b_ = 5 + half
                O3 = PB(b_, 0, 260)[0:NQ, :].rearrange("p (h d) -> p h d", h=4)
                tt("dve", sm[0:NQ, 32 + half * 4: 36 + half * 4].unsqueeze(2), O3[:, :, 64:65],
                   Esink[0:NQ, half * 4:(half + 1) * 4].unsqueeze(2), ALU.add, (bank[b_], bf("esink")), (bf("sm_l0"),))
                recip(sm[0:NQ, 32 + half * 4: 36 + half * 4], sm[0:NQ, 32 + half * 4: 36 + half * 4], (bf("sm_l0"),), (bf("sm_l0"),))
                for hh in range(4):
                    h = half * 4 + hh
                    ts("dve", tmpA[0:NQ, 512 + h * 64: 576 + h * 64], PB(b_, hh * 65, hh * 65 + 64)[0:NQ, :], sm[0:NQ, 32 + h:33 + h], None,
                       ALU.mult, None, (bank[b_], bf("sm_l0")), (bA,))
            tt("dve", MIX[0:NQ, :], MIX[0:NQ, :], tmpA[0:NQ, :], ALU.mult, (bMIX, bA), (bMIX,))
            out_proj(MIX[0:NQ, :], bMIX, W0out, bW, X, bx, x1, bx1, n=NQ)

            rms_to_T(x1[0:NQ, :], bx1, T2, bf("T2"), n=NQ)
            proj(T2, bf("T2"), W1kv, 2064, 0, 512, 1, n=NQ)
            proj(T2, bf("T2"), W1kv, 2064, 512, 512, 2, n=NQ)
            proj(T2, bf("T2"), W1kv, 2064, 1024, 512, 3, n=NQ)
            proj(T2, bf("T2"), W1kv, 2064, 1536, 512, 4, n=NQ)
            proj(T2, bf("T2"), W1kv, 2064, 2048, 16, 5, n=NQ)
            blg = bf("logf")
            tt("dve", logf[0:NQ, :], PB(5, 0, 16)[0:NQ, :], BFG[0:NQ, :], ALU.add, (bank[5], bT), (blg,))
            act(logf[0:NQ, :], logf[0:NQ, :], AF.Exp, (blg,), (blg,), scale=-1.0)
            act(logf[0:NQ, :], logf[0:NQ, :], AF.Ln, (blg, bT), (blg,), bias=ONEC[0:NQ, :])
            ts("dve", logf[0:NQ, :], logf[0:NQ, :], -1.0, None, ALU.mult, None, (blg,), (blg,))
            dma(fls_o[sbi, :, :], logf[0:NQ, :], (blg,), (bf("fls_o"),))
            cp("act", logfn[0:NQ, :], logf[0:NQ, :], (blg,), (bf("logfn"),))
            head_norm(PB(1)[0:NQ, :], (bank[1],), 8, KN1, tmpA[0:NQ, 0:512], bA, 8, n=NQ)
            head_norm(PB(2)[0:NQ, :], (bank[2],), 8, KN1, tmpA[0:NQ, 512:1024], bA, 16, n=NQ)
            dma(fks_o[sbi, :, :], tmpA[0:NQ, :], (bA,), (bf("fks_o"),))
            cp("dve", Gb[0:NQ, :], tmpA[0:NQ, :], (bA,), (bf("Gb"),))
            cp("dve", tmpB[0:NQ, 0:512], PB(3)[0:NQ, :], (bank[3],), (bBt,))
            cp("dve", tmpB[0:NQ, 512:1024], PB(4)[0:NQ, :], (bank[4],), (bBt,))
            dma(fvs_o[sbi, :, :], tmpB[0:NQ, :], (bBt,), (bf("fvs_o"),))
            cp("act", xs[0:NQ, :], tmpB[0:NQ, :], (bBt,), (bf("xs"),))
            dma(Wst[:, :], W_d[0, :, :], (bWd,), (bWst,))
            proj(T2, bf("T2"), Wst, 1024, 0, 512, 1, n=NQ, wbuf=bWst)
            proj(T2, bf("T2"), Wst, 1024, 512, 512, 2, n=NQ, wbuf=bWst)
            head_norm(PB(1)[0:NQ, :], (bank[1],), 8, QN1, tmpA[0:NQ, 0:512], bA, 8, n=NQ)
            head_norm(PB(2)[0:NQ, :], (bank[2],), 8, QN1, tmpA[0:NQ, 512:1024], bA, 16, n=NQ)
            Q3 = v3(Qaug[0:NQ, :], 16)
            cp("dve", Q3[:, :, 0:64], v3(tmpA[0:NQ, :], 16), (bA,), (bQ,))
            dma(Wst[:, :], W_d[1, :, :], (bWd,), (bWst,))
            proj(T2, bf("T2"), Wst, 1024, 0, 512, 1, n=NQ, wbuf=bWst)
            proj(T2, bf("T2"), Wst, 1024, 512, 512, 2, n=NQ, wbuf=bWst)
            silu_gate((1, 2), MIX, bMIX, n=NQ)
            dma(Wst[:, :], W_d[2, :, :], (bWd,), (bWst,))
            memset("dve", carry[:, :], 0.0, (bf("carry"),))
            K3 = v3(Kaug[:, :], 16)
            V3 = v3(Vaug[:, :], 16)
            ts("dve", K3[:, :, 70:71], Ones[:, 0:16].unsqueeze(2), 0.0, None, ALU.mult, None, (bT,), (bK,))
            bc8 = bf("c8")
            bOe = bf("tmpC")

            def obank(h):
                return (6, 7, 1)[h // 7], (h % 7) * 65

            def split_cum(nrow):
                ts("dve", c8[0:nrow, 0:16], cum[0:nrow, :], 8.0, None, ALU.mult, None, (bf("cum"),), (bc8,))
                cp("dve", chi[0:nrow, 0:16], c8[0:nrow, 0:16], (bc8,), (bf("chi"),))
                tt("dve", c8[0:nrow, 16:32], c8[0:nrow, 0:16], chi[0:nrow, 0:16], ALU.subtract, (bc8, bf("chi")), (bc8,))
                cp("dve", chi[0:nrow, 16:32], c8[0:nrow, 16:32], (bc8,), (bf("chi"),))
                tt("dve", c8[0:nrow, 32:48], c8[0:nrow, 16:32], chi[0:nrow, 16:32], ALU.subtract, (bc8, bf("chi")), (bc8,))
                cp("dve", chi[0:nrow, 32:48], c8[0:nrow, 32:48], (bc8,), (bf("chi"),))

            NKT = PAST // 128
            order = [None] + list(range(NKT - 1, -1, -1))
            for it, kt in enumerate(order):
                new = kt is None
                first, lastit = (it == 0), (it == len(order) - 1)
                nk = NQ if new else 128
                if new:
                    cp("act", K3[0:NQ, :, 0:64], v3(Gb[0:NQ, :], 16), (bf("Gb"),), (bK,))
                    cp("dve", V3[0:NQ, :, 0:64], v3(xs[0:NQ, :], 16), (bf("xs"),), (bV,))
                    mm(PB(5, 16, 32)[0:NQ, :], Tri[0:NQ, 0:NQ], logfn[0:NQ, :], True, True, (bT, bf("logfn")), (bank[5],))
                    cp("dve", cum[0:NQ, :], PB(5, 16, 32)[0:NQ, :], (bank[5],), (bf("cum"),))
                else:
                    kst, kstb = xin[it % 2], bf("xin%d" % (it % 2))
                    vst, vstb = (tmpA, bA) if it % 2 == 0 else (tmpB, bBt)
                    dma(kst[:, :], cfk_d[sbi, kt * 128:(kt + 1) * 128, :], (), (kstb,))
                    dma(vst[:, :], cfv_d[sbi, kt * 128:(kt + 1) * 128, :], (), (vstb,))
                    dma(logf[:, :], cfl_d[sbi, kt * 128:(kt + 1) * 128, :], (), (blg,))
                    cp("dve" if it % 2 else "act", K3[:, :, 0:64], v3(kst[:, :], 16), (kstb,), (bK,))
                    cp("act" if it % 2 else "dve", V3[:, :, 0:64], v3(vst[:, :], 16), (vstb,), (bV,))
                    mm(PB(5, 16, 32)[:, :], Tri[:, :], logf[:, :], True, True, (bT, blg), (bank[5],))
                    mm(PB(5, 32, 48)[:, :], Ones[:, :], logf[:, :], True, True, (bT, blg), (bank[5],))
                    tt("dve", cum[:, :], PB(5, 16, 32)[:, :], carry[:, :], ALU.subtract, (bank[5], bf("carry")), (bf("cum"),))
                    tt("dve", cum[:, :], cum[:, :], PB(5, 32, 48)[:, :], ALU.subtract, (bf("cum"), bank[5]), (bf("cum"),))
                    tt("dve", carry[:, :], PB(5, 32, 48)[:, :], carry[:, :], ALU.add, (bank[5], bf("carry")), (bf("carry"),))
                split_cum(nk)
                for i in range(3):
                    ts("dve", K3[0:nk, :, 67 + i:68 + i], chi[0:nk, 16 * i:16 * i + 16].unsqueeze(2), -1.0, None, ALU.mult, None,
                       (bf("chi"),), (bK,))
                if new:
                    for i in range(3):
                        cp("dve", Q3[:, :, 64 + i:65 + i], chi[0:NQ, 16 * i:16 * i + 16].unsqueeze(2), (bf("chi"),), (bQ,))
                    for h in range(16):
                        b_ = 2 + h // 8
                        tr(PBb(b_)[0:KA, (h % 8) * 128:(h % 8) * 128 + NQ], Qaug[0:NQ, h * KA:(h + 1) * KA], (bQ, bT), (bank[b_],))
                    cp("act", v3(QT[0:KA, 0:1024], 8)[:, :, 0:NQ], v3(PBb(2)[0:KA, :], 8)[:, :, 0:NQ], (bank[2],), (bf("QT"),))
                    cp("dve", v3(QT[0:KA, 1024:2048], 8)[:, :, 0:NQ], v3(PBb(3)[0:KA, :], 8)[:, :, 0:NQ], (bank[3],), (bf("QT"),))
                for h in range(16):
                    b_ = 2 + h // 8
                    tr(PBb(b_)[0:KA, (h % 8) * 128:(h % 8) * 128 + nk], Kaug[0:nk, h * KA:(h + 1) * KA], (bK, bT), (bank[b_],))
                ktr = KTr[it % 2]
                bktr = bf("KTr%d" % (it % 2))
                cp("act", v3(ktr[0:KA, 0:1024], 8)[:, :, 0:nk], v3(PBb(2)[0:KA, :], 8)[:, :, 0:nk], (bank[2],), (bktr,))
                cp("dve", v3(ktr[0:KA, 1024:2048], 8)[:, :, 0:nk], v3(PBb(3)[0:KA, :], 8)[:, :, 0:nk], (bank[3],), (bktr,))
                sbk = (0, 4)[it % 2]
                for h in range(16):
                    mm(PB(sbk, h * NQ, (h + 1) * NQ)[0:nk, :], ktr[0:KA, h * 128:h * 128 + nk], QT[0:KA, h * 128:h * 128 + NQ],
                       h == 0, h == 15, (bktr, bf("QT")), (bank[sbk],))
                if new:
                    for h in range(16):
                        o = PB(sbk, h * NQ, (h + 1) * NQ)
                        tt("dve", o[0:NQ, :], o[0:NQ, :], Mc[0:NQ, 0:NQ], ALU.add, (bank[sbk], bT), (bank[sbk],))
                pt = PT[it % 2]
                bpt = PTB[it % 2]
                act(pt[0:nk, 0:512], PB(sbk)[0:nk, :], AF.Exp, (bank[sbk],), bpt, scale=0.125)
                for h in range(16):
                    ob, oc = obank(h)
                    mm(PB(ob, oc, oc + 65)[0:NQ, :], pt[0:nk, h * NQ:(h + 1) * NQ], Vaug[0:nk, h * 65:(h + 1) * 65],
                       first and h in (0, 7, 14), lastit, bpt + (bV,), (bank[ob],))
            for gi, (ob, nh) in enumerate(((6, 7), (7, 7), (1, 2))):
                cp("act" if gi != 1 else "dve", Oe[0:NQ, gi * 455: gi * 455 + nh * 65], PB(ob, 0, nh * 65)[0:NQ, :], (bank[ob],), (bOe,))
            O3 = v3(Oe[0:NQ, :], 16)
            recip(sm[0:NQ, 64:80].unsqueeze(2), O3[:, :, 64:65], (bOe,), (bf("sm_l1"),))
            for h in range(16):
                ts("dve", tmpA[0:NQ, h * 64:(h + 1) * 64], Oe[0:NQ, h * 65:h * 65 + 64], sm[0:NQ, 64 + h:65 + h], None,
                   ALU.mult, None, (bOe, bf("sm_l1")), (bA,))
            tt("dve", MIX[0:NQ, :], MIX[0:NQ, :], tmpA[0:NQ, :], ALU.mult, (bMIX, bA), (bMIX,))
            out_proj(MIX[0:NQ, :], bMIX, Wst, bWst, x1, bx1, tmpB, bBt, n=NQ)
            dma(ys_o[sbi, :, :], tmpB[0:NQ, :], (bBt,), (bf("ys_o"),))

        P.finalize()
        P.emit(nc, block, esem, dsem)
    return nc


def _tables():
    t = np.arange(128)
    W = (2, 4, 8, 16)
    dgen = np.zeros((128, 4, 128), np.float32)
    dfirst = np.zeros((128, 4, 128), np.float32)
    dprev = np.zeros((128, 4, 128), np.float32)
    for g, w in enumerate(W):
        inwin = (t[:, None] <= t[None, :]) & (t[:, None] > t[None, :] - w)
        dgen[:, g, :] = inwin / float(w) - np.eye(128)
        dfirst[:, g, :] = inwin / np.minimum(t[None, :] + 1, w).astype(np.float32) - np.eye(128)
        dist = 128 + t[None, :] - t[:, None]
        dprev[:, g, :] = (dist < w) / float(w)
    BIG = 1e7
    tq, kj = t[None, :], t[:, None]
    cq, ck = tq // 64, kj // 64
    aprev = np.abs(128 + tq - kj).astype(np.float32) + BIG * ((cq == 1) & (ck == 0))
    aown = np.abs(tq - kj).astype(np.float32) + BIG * ((cq == 0) & (ck == 1))
    mc = np.where(kj > tq, NEG, 0.0).astype(np.float32)
    tri = (kj <= tq).astype(np.float32)
    return dgen, dfirst, dprev, aprev.astype(np.float32), aown.astype(np.float32), mc, tri


def kernel(x_prompt, x_sample, state_pool, cache_swa_k, cache_swa_v, cache_fox_k, cache_fox_v, cache_fox_logf,
           norm0_g, w_in0, w_pool, pool_scale, swa_qn_g, swa_kn_g, swa_sinks, w_out0,
           norm1_g, w_in1, b_forget, fox_qn_g, fox_kn_g, w_out1, _nt=None, _raw=False):
    NT = NT_FULL if _nt is None else _nt
    f32 = lambda a: np.ascontiguousarray(np.asarray(a), dtype=np.float32)
    bf = ml_dtypes.bfloat16
    xpr = f32(x_prompt)[0]
    dgen, dfirst, dprev, aprev, aown, mc, tri = _tables()
    vecs = np.zeros((1, 1024), np.float32)
    vecs[0, 0:512] = f32(pool_scale)
    vecs[0, 512:576] = f32(swa_qn_g); vecs[0, 576:640] = f32(swa_kn_g)
    vecs[0, 640:648] = f32(swa_sinks)
    vecs[0, 656:672] = f32(b_forget)
    vecs[0, 672:736] = f32(fox_qn_g); vecs[0, 736:800] = f32(fox_kn_g)
    common = {
        "dprev": dprev.reshape(128, 512).astype(bf), "aprev": aprev, "aown": aown, "mc": np.ascontiguousarray(np.tile(mc, (1, 4))), "tri": tri,
        "ones": np.ones((128, 128), np.float32), "ident": np.eye(128, dtype=np.float32).astype(bf),
        "w_in0": f32(w_in0), "w_out0": f32(w_out0), "w_in1": f32(w_in1), "w_out1": f32(w_out1), "w_pool": f32(w_pool),
        "g0c": np.ascontiguousarray(f32(norm0_g).reshape(8, 128).T), "g1c": np.ascontiguousarray(f32(norm1_g).reshape(8, 128).T),
        "vecs": vecs,
    }
    in_maps = []
    for c in range(NCORES):
        pad = 7 - c
        nreal = NT - pad
        xp = np.zeros((NT * 128, D), np.float32)
        xp[pad * 128:] = xpr[: nreal * 128]
        kval = np.zeros((128, NT), np.float32)
        kval[:, :pad] = NEG
        dsel = np.broadcast_to(dgen.reshape(1, 128, 512), (9, 128, 512)).copy()
        dsel[pad] = dfirst.reshape(128, 512)
        m = dict(common)
        m.update({"xp": xp, "kval": kval, "dsel": dsel.astype(bf)})
        sl = slice(2 * c, 2 * c + 2)
        m.update({
            "xs_in": f32(x_sample)[sl], "spool": f32(state_pool)[sl],
            "cswk": f32(cache_swa_k)[sl].reshape(2, 128, 128), "cswv": f32(cache_swa_v)[sl].reshape(2, 128, 128),
            "cfk": f32(cache_fox_k)[sl].reshape(2, 4096, D), "cfv": f32(cache_fox_v)[sl].reshape(2, 4096, D),
            "cfl": f32(cache_fox_logf)[sl],
        })
        in_maps.append(m)
    nc = build_program(NT)
    res = run_bass_kernel_spmd(nc, in_maps, core_ids=list(range(NCORES)))
    R = res.results
    if _raw:
        return R
    NS = NT // 8
    ntok = NT * 128
    yp = np.zeros((1, 16384, D), np.float32)
    fk = np.zeros((1, 16384, 16, 64), np.float32)
    fv = np.zeros((1, 16384, 16, 64), np.float32)
    fl = np.zeros((1, 16384, 16), np.float32)
    for c in range(NCORES):
        for j in range(NS):
            r = 8 * j + c
            yp[0, r * 128:(r + 1) * 128] = np.asarray(R[c]["y_o"])[j]
            fk[0, r * 128:(r + 1) * 128] = np.asarray(R[c]["fk_o"])[j].reshape(128, 16, 64)
            fv[0, r * 128:(r + 1) * 128] = np.asarray(R[c]["fv_o"])[j].reshape(128, 16, 64)
            fl[0, r * 128:(r + 1) * 128] = np.asarray(R[c]["fl_o"])[j]
    pool_p = np.asarray(R[7]["pool_o"]).reshape(1, 15, 512).astype(np.float32)
    swk_p = np.asarray(R[7]["swk_o"]).reshape(1, 128, 2, 64).astype(np.float32)
    swv_p = np.asarray(R[7]["swv_o"]).reshape(1, 128, 2, 64).astype(np.float32)
    cat = lambda k: np.concatenate([np.asarray(R[c][k], dtype=np.float32) for c in range(NCORES)], axis=0)
    ys = cat("ys_o")
    pool_s = cat("pools_o")
    swk_s = cat("swks_o").reshape(16, 128, 2, 64)
    swv_s = cat("swvs_o").reshape(16, 128, 2, 64)
    fk_s = cat("fks_o").reshape(16, 32, 16, 64)
    fv_s = cat("fvs_o").reshape(16, 32, 16, 64)
    fl_s = cat("fls_o")
    return (yp, ys, pool_p, pool_s, swk_p, swv_p, swk_s, swv_s, fk, fv, fl, fk_s, fv_s, fl_s)
```

```python
import numpy as np
import ml_dtypes
from contextlib import ExitStack
import concourse.bass as bass
import concourse.mybir as mybir
from concourse.bass_utils import run_bass_kernel_spmd

F32 = mybir.dt.float32
BF16 = mybir.dt.bfloat16
AF = mybir.ActivationFunctionType
ALU = mybir.AluOpType
AX = mybir.AxisListType

NCORES = 8
D = 1024
NT_FULL = 128
EPS = 1e-6
NEG = -30000.0
KA = 71
SYNC_SAME_ENGINE = True
KSTOP = 99.0


class Buf:
    __slots__ = ("name", "w", "r", "excl")

    def __init__(self, name, excl=False):
        self.name = name
        self.w = None
        self.r = []
        self.excl = excl


class Op:
    __slots__ = ("eng", "fn", "deps", "sig", "ticket", "dma", "slot", "idx")


class Prog:
    ENG = ("pe", "act", "dve", "pool", "sp")
    NRING = 12

    def __init__(self):
        self.ops = []
        self.ndma = 0
        self.ring_last = [None] * self.NRING
        self.capture = None

    def op(self, eng, fn, r=(), w=(), dma=False):
        if self.capture is not None:
            self.capture.append((eng, fn, tuple(r), tuple(w), dma))
            return None
        return self.commit(eng, fn, r, w, dma)

    def captured(self, emit_fn):
        assert self.capture is None
        self.capture = []
        emit_fn()
        out, self.capture = self.capture, None
        return out

    @staticmethod
    def interleave(A, B):
        out, i, j, na, nb = [], 0, 0, len(A), len(B)
        while i < na or j < nb:
            if j >= nb or (i < na and i * nb <= j * na):
                out.append(A[i]); i += 1
            else:
                out.append(B[j]); j += 1
        return out

    def commit(self, eng, fn, r=(), w=(), dma=False):
        o = Op()
        o.eng, o.fn, o.sig, o.ticket, o.dma, o.slot = eng, fn, False, None, dma, None
        o.idx = len(self.ops)
        deps = {}

        def add(d, kind):
            if d is not o:
                deps.setdefault(d, set()).add(kind)

        for b in r:
            if b in w:
                continue
            if b.w is not None:
                add(b.w, "RAW")
            if b.excl:
                for x in b.r:
                    add(x, "XRD")
        for b in w:
            if b.w is not None:
                add(b.w, "RAW" if b in r else "WAW")
            for x in b.r:
                add(x, "WAR")
        for b in r:
            if b not in w:
                b.r.append(o)
        for b in w:
            b.w = o
            b.r = []
        if dma:
            o.slot = self.ndma % self.NRING
            prev = self.ring_last[o.slot]
            if prev is not None:
                add(prev, "RING")
            self.ring_last[o.slot] = o
            self.ndma += 1
        o.deps = deps
        self.ops.append(o)
        return o

    def finalize(self):
        for o in self.ops:
            keep = set()
            for d, kinds in o.deps.items():
                if d.eng == o.eng and not d.dma:
                    if o.eng in ("pe", "sp"):
                        continue
                    if not SYNC_SAME_ENGINE:
                        continue
                    if o.eng != "pool" and not (kinds & {"RAW", "WAR"}):
                        continue
                d.sig = True
                keep.add(d)
            o.deps = keep
        cnt = {e: 0 for e in self.ENG}
        rc = [0] * self.NRING
        for o in self.ops:
            if o.dma:
                rc[o.slot] += 16
                o.ticket = rc[o.slot]
            elif o.sig:
                cnt[o.eng] += 1
                o.ticket = cnt[o.eng]

    def emit(self, nc, block, esem, dsem):
        per = {e: [o for o in self.ops if o.eng == e] for e in self.ENG}

        def run(ename, eng):
            waited = {}
            for o in per[ename]:
                need = {}
                for d in o.deps:
                    key = ("d", d.slot) if d.dma else ("e", d.eng)
                    if d.ticket > need.get(key, 0):
                        need[key] = d.ticket
                for key, val in need.items():
                    if waited.get(key, 0) >= val:
                        continue
                    waited[key] = val
                    sem = dsem[key[1]] if key[0] == "d" else esem[key[1]]
                    eng.wait_ge(sem, val)
                ins = o.fn(eng)
                if o.dma:
                    ins.then_inc(dsem[o.slot], 16)
                elif o.sig:
                    ins.then_inc(esem[ename], 1)
            if ename == "sp":
                for k, last in enumerate(self.ring_last):
                    if last is not None:
                        eng.wait_ge(dsem[k], last.ticket)

        @block.tensor
        def _(e):
            run("pe", e)

        @block.scalar
        def _(e):
            run("act", e)

        @block.vector
        def _(e):
            run("dve", e)

        @block.gpsimd
        def _(e):
            run("pool", e)

        @block.sync
        def _(e):
            run("sp", e)


def build_program(NT):
    NSLOT = NT // 8
    nc = bass.Bass("TRN2", target_bir_lowering=False)
    P = Prog()

    def din(name, shape, dt=F32):
        return nc.dram_tensor(name, list(shape), dt, kind="ExternalInput")

    def dout(name, shape, dt=F32):
        return nc.dram_tensor(name, list(shape), dt, kind="ExternalOutput")

    xp = din("xp", [NT * 128, D])
    kval_d = din("kval", [128, NT])
    dsel_d = din("dsel", [9, 128, 512], BF16)
    dprev_d = din("dprev", [128, 512], BF16)
    aprev_d = din("aprev", [128, 128])
    aown_d = din("aown", [128, 128])
    mc_d = din("mc", [128, 512])
    tri_d = din("tri", [128, 128])
    ones_d = din("ones", [128, 128])
    ident_d = din("ident", [128, 128], BF16)
    w_in0 = din("w_in0", [D, 2304])
    w_out0 = din("w_out0", [D, D])
    w_in1 = din("w_in1", [D, 4112])
    w_out1 = din("w_out1", [D, D])
    w_pool = din("w_pool", [4, 128, 128])
    g0_d = din("g0c", [128, 8])
    g1_d = din("g1c", [128, 8])
    vecs_d = din("vecs", [1, 1024])

    xs_d = din("xs_in", [2, 32, D])
    spool_d = din("spool", [2, 15, 512])
    cswk_d = din("cswk", [2, 128, 128])
    cswv_d = din("cswv", [2, 128, 128])
    cfk_d = din("cfk", [2, 4096, D])
    cfv_d = din("cfv", [2, 4096, D])
    cfl_d = din("cfl", [2, 4096, 16])
    ys_o = dout("ys_o", [2, 32, D])
    pools_o = dout("pools_o", [2, 15, 512])
    swks_o = dout("swks_o", [2, 128, 128])
    swvs_o = dout("swvs_o", [2, 128, 128])
    fks_o = dout("fks_o", [2, 32, D])
    fvs_o = dout("fvs_o", [2, 32, D])
    fls_o = dout("fls_o", [2, 32, 16])

    y_o = dout("y_o", [NSLOT, 128, D])
    fk_o = dout("fk_o", [NSLOT, 128, D])
    fv_o = dout("fv_o", [NSLOT, 128, D])
    fl_o = dout("fl_o", [NSLOT, 128, 16])
    pool_o = dout("pool_o", [15, 512])
    swk_o = dout("swk_o", [128, 128])
    swv_o = dout("swv_o", [128, 128])

    KTP = 96
    KT_d = nc.dram_tensor("KT_d", [NT, KTP, 2048], BF16)
    V_d = nc.dram_tensor("V_d", [NT, 128, 1040], BF16)
    W_d = nc.dram_tensor("W_d", [3, 128, 8192], BF16)

    es = ExitStack()
    with es:
        def sb(name, shape, dt=F32):
            return es.enter_context(nc.sbuf_tensor(name, list(shape), dt))

        W0in = sb("W0in", [128, 8 * 2304], BF16)
        W0out = sb("W0out", [128, 8 * 1024], BF16)
        W1kv = sb("W1kv", [128, 8 * 2064], BF16)
        Wpool = sb("Wpool", [128, 512], BF16)
        Wst = sb("Wst", [128, 8192], BF16)
        Aprev = sb("Aprev", [128, 128]); Aown = sb("Aown", [128, 128]); Mc = sb("Mc", [128, 512])
        Tri = sb("Tri", [128, 128]); Ones = sb("Ones", [128, 128]); Ident = sb("Ident", [128, 128], BF16)
        Dprev = sb("Dprev", [128, 512], BF16); Dcur = sb("Dcur", [128, 512], BF16)
        Kval = sb("Kval", [128, NT])
        G0c = sb("G0c", [128, 8]); G1c = sb("G1c", [128, 8])
        Vecs = sb("Vecs", [128, 800])
        Esink = sb("Esink", [128, 8])
        EPSC = sb("EPSC", [128, 1]); ONEC = sb("ONEC", [128, 1])
        xin = [sb("xin0", [128, D]), sb("xin1", [128, D])]
        tmpA = sb("tmpA", [128, D]); tmpB = sb("tmpB", [128, D])
        MIXb = [sb("MIX0", [128, D]), sb("MIX1", [128, D])]
        MIX = MIXb[0]
        tmpE = sb("tmpE", [128, D])
        x1b = [sb("x1_0", [128, D]), sb("x1_1", [128, D])]
        x1 = x1b[0]
        tmpC = sb("tmpC", [128, 1040]); tmpD = sb("tmpD", [128, D])
        xs = sb("xs", [128, D], BF16)
        xsB = sb("xsB", [128, D], BF16)
        T1 = sb("T1", [128, D], BF16); T2 = sb("T2", [128, D], BF16)
        QT = sb("QT", [128, 2048], BF16)
        PT = [sb("PT0", [128, D], BF16), sb("PT1", [128, D], BF16)]
        Gb = sb("Gb", [128, D], BF16)
        KTo = sb("KTo", [128, 2048], BF16)
        Qaug = sb("Qaug", [128, 16 * KA], BF16)
        Kaug = sb("Kaug", [128, 16 * KA], BF16)
        Vaug = sb("Vaug", [128, 1040], BF16)
        q0aug2 = [sb("q0aug", [128, 8 * 65], BF16), sb("q0augB", [128, 8 * 65], BF16)]
        k0aug2 = [sb("k0aug", [128, 2 * 65], BF16), sb("k0augB", [128, 2 * 65], BF16)]
        q0aug, k0aug = q0aug2[0], k0aug2[0]
        v0aug = [sb("v0aug%d" % i, [128, 130], BF16) for i in range(3)]
        KT0 = [sb("KT0a", [128, 256], BF16), sb("KT0b", [128, 256], BF16)]
        ubf = [sb("ubf%d" % i, [128, 512], BF16) for i in range(3)]
        diffT = sb("diffT", [128, 512], BF16)
        KTr = [sb("KTr%d" % i, [128, 2048], BF16) for i in range(2)]
        Vr = [sb("Vr%d" % i, [128, 1040], BF16) for i in range(2)]
        Oe = tmpC
        sm = sb("sm", [128, 256])
        cum = sb("cum", [128, 16]); carry = sb("carry", [128, 16]); logf = sb("logf", [128, 16]); logfn = sb("logfn", [128, 16])
        c8 = sb("c8", [128, 64]); chi = sb("chi", [128, 48], BF16)
        psb = [es.enter_context(nc.psum_tensor("ps%d" % i, [128, 512], F32)) for i in range(8)]
        esem = {e: es.enter_context(nc.semaphore("s_" + e)) for e in Prog.ENG}
        dsem = [es.enter_context(nc.semaphore("d%d" % i)) for i in range(Prog.NRING)]
        block = es.enter_context(nc.Block())

        B = {}

        def bf(name):
            if name not in B:
                B[name] = Buf(name)
            return B[name]

        bank = [bf("bank%d" % i) for i in range(8)]
        PTB = [(bf("PTq0"), bf("PTq1")), (bf("PTq2"), bf("PTq3"))]
        for b_ in bank:
            b_.excl = True

        def PB(b, lo=0, hi=512):
            return psb[b][:, lo:hi]

        def PBb(b):
            return psb[b][:, :].bitcast(BF16)

        def dma(out, in_, r, w):
            return P.op("sp", lambda e: e.dma_start(out=out, in_=in_), r, w, dma=True)

        def mm(out, lhsT, rhs, start, stop, r, w):
            return P.op("pe", lambda e: e.matmul(out, lhsT, rhs, start=start, stop=stop,
                                                 skip_group_check=True), r, w)

        def tr(out, in_, r, w):
            n = in_.shape[0]
            return P.op("pe", lambda e: e.transpose(out, in_, Ident[0:n, 0:n]), r, w)

        def act(out, in_, func, r, w, scale=1.0, bias=0.0, accum=None):
            if accum is None:
                return P.op("act", lambda e: e.activation(out, in_, func, bias=bias, scale=scale), r, w)
            return P.op("act", lambda e: e.activation(out, in_, func, bias=bias, scale=scale,
                                                      accum_out=accum), r, w)

        def ts(eng, out, in0, s1, s2, op0, op1, r, w):
            if op1 is None:
                return P.op(eng, lambda e: e.tensor_scalar(out, in0, s1, None, op0), r, w)
            return P.op(eng, lambda e: e.tensor_scalar(out, in0, s1, s2, op0, op1), r, w)

        def tt(eng, out, in0, in1, op, r, w):
            return P.op(eng, lambda e: e.tensor_tensor(out, in0, in1, op), r, w)

        def stt(eng, out, in0, scalar, in1, op0, op1, r, w):
            return P.op(eng, lambda e: e.scalar_tensor_tensor(out, in0, scalar, in1, op0, op1), r, w)

        def cp(eng, out, in_, r, w):
            if eng == "act":
                return P.op("act", lambda e: e.copy(out, in_), r, w)
            return P.op(eng, lambda e: e.tensor_copy(out, in_), r, w)

        def memset(eng, ap, val, w):
            return P.op(eng, lambda e: e.memset(ap, val), (), w)

        def red(eng, out, in_, r, w):
            return P.op(eng, lambda e: e.tensor_reduce(out, in_, AX.X, ALU.add), r, w)

        def recip(out, in_, r, w):
            return P.op("dve", lambda e: e.reciprocal(out, in_), r, w)

        def v3(ap, h):
            return ap.rearrange("p (h d) -> p h d", h=h)

        bT = bf("tables")
        for t_sb, t_d in ((Aprev, aprev_d), (Aown, aown_d), (Mc, mc_d), (Tri, tri_d), (Ones, ones_d),
                          (Ident, ident_d), (Dprev, dprev_d), (Kval, kval_d), (G0c, g0_d), (G1c, g1_d)):
            dma(t_sb[:, :], t_d[:, :], (), (bT,))
        dma(Vecs[:, :], vecs_d[0:1, 0:800].partition_broadcast(128), (), (bT,))
        PSC = Vecs[:, 0:512]
        QN0 = Vecs[:, 512:576]; KN0 = Vecs[:, 576:640]
        BFG = Vecs[:, 656:672]
        QN1 = Vecs[:, 672:736]; KN1 = Vecs[:, 736:800]
        memset("dve", EPSC[:, :], EPS, (bT,))
        memset("dve", ONEC[:, :], 1.0, (bT,))
        act(Esink[:, :], Vecs[:, 640:648], AF.Exp, (bT,), (bf("esink"),))

        bq0, bk0, bQ, bK, bV = bf("q0aug"), bf("k0aug"), bf("Qaug"), bf("Kaug"), bf("Vaug")
        memset("pool", q0aug2[0][:, :], 1.0, (bq0,))
        memset("pool", q0aug2[1][:, :], 1.0, (bf("q0augB"),))
        memset("pool", v0aug[0][:, :], 1.0, (bf("v0aug0"),))
        memset("pool", v0aug[1][:, :], 1.0, (bf("v0aug1"),))
        memset("pool", v0aug[2][:, :], 0.0, (bf("v0aug2"),))
        memset("pool", ubf[2][:, :], 0.0, (bf("ubf2"),))
        memset("pool", KT0[1][:, :], 0.0, (bf("KT0b"),))
        memset("pool", KT0[1][64:65, :], NEG, (bf("KT0b"),))
        memset("pool", Qaug[:, :], 1.0, (bQ,))
        memset("pool", KTo[:, :], 0.0, (bf("KTo"),))
        memset("pool", Kaug[:, :], 1.0, (bK,))
        memset("pool", Vaug[:, :], 1.0, (bV,))
        memset("pool", carry[:, :], 0.0, (bf("carry"),))
        memset("dve", v0aug[2][:, 64:65], 1.0, (bf("v0aug2"),))
        memset("dve", v0aug[2][:, 129:130], 1.0, (bf("v0aug2"),))

        bA, bBt = bf("tmpA"), bf("tmpB")
        stage = [(tmpA, bA), (tmpB, bBt)]
        cnt = [0]

        def load_w(src, ncols_total, c0, c1, dst, dst_stride, dst_off, gcol, dst_buf):
            for kc in range(8):
                for a in range(c0, c1, 1024):
                    b_ = min(a + 1024, c1)
                    st, sbuf = stage[cnt[0] % 2]
                    eng = "dve"
                    cnt[0] += 1
                    dma(st[:, 0:b_ - a], src[kc * 128:(kc + 1) * 128, a:b_], (), (sbuf,))
                    o = dst[:, kc * dst_stride + dst_off + (a - c0): kc * dst_stride + dst_off + (b_ - c0)]
                    if gcol is None:
                        cp(eng, o, st[:, 0:b_ - a], (sbuf,), (dst_buf,))
                    else:
                        ts(eng, o, st[:, 0:b_ - a], gcol[:, kc:kc + 1], None, ALU.mult, None,
                           (sbuf, bT), (dst_buf,))

        bW = bf("weights")
        bWst = bf("Wst")
        load_w(w_in0, 2304, 0, 2304, W0in, 2304, 0, G0c, bW)
        load_w(w_out0, 1024, 0, 1024, W0out, 1024, 0, None, bW)
        load_w(w_in1, 4112, 1024, 3072, W1kv, 2064, 0, G1c, bW)
        load_w(w_in1, 4112, 4096, 4112, W1kv, 2064, 2048, G1c, bW)
        for g in range(4):
            st, sbuf = stage[g % 2]
            dma(st[:, 0:128], w_pool[g, :, :], (), (sbuf,))
            cp("dve", Wpool[:, g * 128:(g + 1) * 128], st[:, 0:128], (sbuf,), (bW,))
        bWd = bf("W_d")
        for i, (src, c0, gcol) in enumerate(((w_in1, 0, G1c), (w_in1, 3072, G1c), (w_out1, 0, None))):
            load_w(src, 0, c0, c0 + 1024, Wst, 1024, 0, gcol, bWst)
            dma(W_d[i, :, :], Wst[:, :], (bWst,), (bWd,))

        bsm = bf("sm")

        def rms_to_T(src_ap, src_buf, dstT, dstT_buf, n=128, xs_t=None, xs_n="xs", smc=0, bt=0):
            xs_t = xs if xs_t is None else xs_t
            bxs, bsr = bf(xs_n), bf("sm_rms%d" % smc)
            act(xs_t[0:n, :], src_ap, AF.Square, (src_buf,), (bxs, bsr), accum=sm[0:n, smc:smc + 1])
            act(sm[0:n, smc + 1:smc + 2], sm[0:n, smc:smc + 1], AF.Ln, (bsr, bT), (bsr,), scale=1.0 / D, bias=EPSC[0:n, :])
            act(sm[0:n, smc + 1:smc + 2], sm[0:n, smc + 1:smc + 2], AF.Exp, (bsr,), (bsr,), scale=-0.5)
            ts("dve", xs_t[0:n, :], src_ap, sm[0:n, smc + 1:smc + 2], None, ALU.mult, None, (src_buf, bsr), (bxs,))
            for kc in range(8):
                tr(PBb(bt)[:, kc * 128: kc * 128 + n], xs_t[0:n, kc * 128:(kc + 1) * 128], (bxs, bT), (bank[bt],))
            if n == 128:
                cp("act", dstT[:, :], PBb(bt)[:, :], (bank[bt],), (dstT_buf,))
            else:
                cp("act", v3(dstT[:, :], 8)[:, :, 0:n], v3(PBb(bt)[:, :], 8)[:, :, 0:n], (bank[bt],), (dstT_buf,))

        FILL = [0]

        def filler():
            for _ in range(FILL[0]):
                mm(PB(2)[:, :], Ident[:, :], W0out[:, 0:512], True, True, (bT, bW), (bank[2],))

        def proj(hT, hT_buf, W, wstride, c0, ncols, b, n=128, wbuf=None):
            for kc in range(8):
                mm(PB(b, 0, ncols)[0:n, :], hT[:, kc * 128: kc * 128 + n], W[:, kc * wstride + c0: kc * wstride + c0 + ncols],
                   kc == 0, kc == 7, (hT_buf, wbuf or bW), (bank[b],))
            filler()

        def head_norm(src_ap, src_bufs, nh, gain, out_f32, out_buf, smcol, n=128, scr=None, scr_b=None):
            scr = tmpB if scr is None else scr
            scr_b = bBt if scr_b is None else scr_b
            bs = bf("sm_hn%d" % smcol)
            smv = sm[0:n, smcol:smcol + nh]
            act(scr[0:n, 0:nh * 64], src_ap, AF.Square, src_bufs, (scr_b,))
            red("dve", smv, v3(scr[0:n, 0:nh * 64], nh), (scr_b,), (bs,))
            act(smv, smv, AF.Ln, (bs, bT), (bs,), scale=1.0 / 64, bias=EPSC[0:n, :])
            act(smv, smv, AF.Exp, (bs,), (bs,), scale=-0.5)
            tt("dve", v3(out_f32, nh), v3(src_ap, nh), smv.unsqueeze(2).to_broadcast([n, nh, 64]), ALU.mult,
               tuple(src_bufs) + (bs,), (out_buf,))
            tt("dve", v3(out_f32, nh), v3(out_f32, nh), gain[0:n, :].unsqueeze(1).to_broadcast([n, nh, 64]), ALU.mult,
               (out_buf, bT), (out_buf,))

        def silu_gate(gate_banks, out_ap, out_buf, n=128):
            for i, b in enumerate(gate_banks):
                sl = slice(i * 512, (i + 1) * 512)
                act(out_ap[0:n, sl], PB(b)[0:n, :], AF.Exp, (bank[b],), (out_buf,), scale=-1.0)
            act(out_ap[0:n, :], out_ap[0:n, :], AF.Ln, (out_buf, bT), (out_buf,), bias=ONEC[0:n, :])
            act(out_ap[0:n, :], out_ap[0:n, :], AF.Exp, (out_buf,), (out_buf,), scale=-1.0)
            for i, b in enumerate(gate_banks):
                sl = slice(i * 512, (i + 1) * 512)
                tt("dve", out_ap[0:n, sl], out_ap[0:n, sl], PB(b)[0:n, :], ALU.mult, (out_buf, bank[b]), (out_buf,))

        def out_proj(G_ap, G_buf, W, wbuf, res_ap, res_buf, dst_ap, dst_buf, n=128, bt=0, bo=(2, 3), T1=T1, t1n="T1"):
            bf_T1 = bf(t1n)
            cp("act", Gb[0:n, :], G_ap, (G_buf,), (bf("Gb"),))
            for kc in range(8):
                tr(PBb(bt)[:, kc * 128: kc * 128 + n], Gb[0:n, kc * 128:(kc + 1) * 128], (bf("Gb"), bT), (bank[bt],))
            if n == 128:
                cp("act", T1[:, :], PBb(bt)[:, :], (bank[bt],), (bf_T1,))
            else:
                cp("act", v3(T1[:, :], 8)[:, :, 0:n], v3(PBb(bt)[:, :], 8)[:, :, 0:n], (bank[bt],), (bf_T1,))
            for half in range(2):
                b = bo[half]
                for kc in range(8):
                    mm(PB(b)[0:n, :], T1[:, kc * 128: kc * 128 + n], W[:, kc * 1024 + half * 512: kc * 1024 + (half + 1) * 512],
                       kc == 0, kc == 7, (bf_T1, wbuf), (bank[b],))
                filler()
                tt("dve", dst_ap[0:n, half * 512:(half + 1) * 512], PB(b)[0:n, :], res_ap[0:n, half * 512:(half + 1) * 512],
                   ALU.add, (bank[b], res_buf), (dst_buf,))

        slopes = [2.0 ** (-(h + 1)) for h in range(8)]

        bMIX = bf("MIX0")
        bC, bDt = bf("tmpC"), bf("tmpD")

        def L0a(v):
            bxin = bf("xin%d" % (v % 2))
            X = xin[v % 2]
            q0a, bq = q0aug2[v % 2], bf("q0aug" if v % 2 == 0 else "q0augB")
            k0a, bk = k0aug2[v % 2], bf("k0aug" if v % 2 == 0 else "k0augB")
            u3, bu = ubf[v % 3], bf("ubf%d" % (v % 3))
            v3a, bv0 = v0aug[v % 3], bf("v0aug%d" % (v % 3))
            Mx, bMx = MIXb[v % 2], bf("MIX%d" % (v % 2))
            dma(X[:, :], xp[v * 128:(v + 1) * 128, :], (), (bxin,))
            rms_to_T(X[:, :], bxin, T1, bf("T1"), bt=0)
            proj(T1, bf("T1"), W0in, 2304, 0, 512, 0)
            proj(T1, bf("T1"), W0in, 2304, 512, 512, 1)
            cp("act", u3[:, :], PB(0)[:, :], (bank[0],), (bu,))
            if v == NT - 1:
                cp("dve", tmpA[:, 0:512], PB(0)[:, :], (bank[0],), (bA,))
                dma(pool_o[:, :], tmpA[113:128, 0:512], (bA,), (bf("pool_o"),))
            proj(T1, bf("T1"), W0in, 2304, 1024, 256, 0)
            head_norm(PB(1)[:, :], (bank[1],), 8, QN0, tmpA[:, 0:512], bA, 8)
            cp("dve", v3(q0a[:, :], 8)[:, :, 0:64], v3(tmpA[:, 0:512], 8), (bA,), (bq,))
            proj(T1, bf("T1"), W0in, 2304, 1280, 512, 1)
            head_norm(PB(0, 0, 128)[:, :], (bank[0],), 2, KN0, tmpA[:, 512:640], bA, 16)
            cp("act", v3(k0a[:, :], 2)[:, :, 0:64], v3(tmpA[:, 512:640], 2), (bA,), (bk,))
            ts("dve", v3(k0a[:, :], 2)[:, :, 64:65], Ones[:, 0:2].unsqueeze(2), Kval[:, v:v + 1], None, ALU.mult, None, (bT,), (bk,))
            cp("act", v3(v3a[:, :], 2)[:, :, 0:64], v3(PB(0, 128, 256)[:, :], 2), (bank[0],), (bv0,))
            if v == NT - 1:
                dma(swk_o[:, :], tmpA[:, 512:640], (bA,), (bf("swk_o"),))
                cp("dve", tmpA[:, 640:768], PB(0, 128, 256)[:, :], (bank[0],), (bA,))
                dma(swv_o[:, :], tmpA[:, 640:768], (bA,), (bf("swv_o"),))
            proj(T1, bf("T1"), W0in, 2304, 1792, 512, 0)
            silu_gate((1, 0), Mx, bMx)

        def L0b(v):
            cur, prv = v % 2, (v + 1) % 2
            bxin = bf("xin%d" % cur)
            X = xin[cur]
            bx1 = bf("x1_%d" % cur)
            q0a, bq = q0aug2[cur], bf("q0aug" if cur == 0 else "q0augB")
            k0a, bk = k0aug2[cur], bf("k0aug" if cur == 0 else "k0augB")
            uc, buc = ubf[v % 3], bf("ubf%d" % (v % 3))
            up, bup = ubf[(v - 1) % 3], bf("ubf%d" % ((v - 1) % 3))
            vc, bvc = v0aug[v % 3], bf("v0aug%d" % (v % 3))
            vp, bvp = v0aug[(v - 1) % 3], bf("v0aug%d" % ((v - 1) % 3))
            Mx, bMx = MIXb[cur], bf("MIX%d" % cur)
            bE = bf("tmpE")
            T3 = QT[:, 1024:2048]
            if v < 9:
                dma(Dcur[:, :], dsel_d[v, :, :], (), (bf("Dcur"),))
            for h in range(8):
                tr(PBb(5)[0:65, h * 128:(h + 1) * 128], q0a[:, h * 65:(h + 1) * 65], (bq, bT), (bank[5],))
            cp("act", QT[0:65, 0:1024], PBb(5)[0:65, :], (bank[5],), (bf("QT"),))
            bkt = bf("KT0%s" % "ab"[cur])
            for g in range(2):
                tr(PBb(3)[0:65, g * 128:(g + 1) * 128], k0a[:, g * 65:(g + 1) * 65], (bk, bT), (bank[3],))
            cp("dve", KT0[cur][0:65, :], PBb(3)[0:65, 0:256], (bank[3],), (bkt,))
            for g in range(4):
                mm(PB(4, g * 128, (g + 1) * 128)[:, :], uc[:, g * 128:(g + 1) * 128], Dcur[:, g * 128:(g + 1) * 128],
                   g == 0, False, (buc, bf("Dcur")), (bank[4],))
                mm(PB(4, g * 128, (g + 1) * 128)[:, :], up[:, g * 128:(g + 1) * 128], Dprev[:, g * 128:(g + 1) * 128],
                   False, True, (bup, bT), (bank[4],))
            cp("act", diffT[:, :], PB(4)[:, :], (bank[4],), (bf("diffT"),))
            sbs = (5, 3)

            def s_block(kt_i, ktb, ktbuf, A):
                for g in range(2):
                    mm(PB(sbs[g])[:, :], ktb[0:65, g * 128:(g + 1) * 128], QT[0:65, g * 512:(g + 1) * 512],
                       True, True, (ktbuf, bf("QT")), (bank[sbs[g]],))
                if kt_i == 0:
                    for g in range(4):
                        mm(PB(4, g * 128, (g + 1) * 128)[:, :], diffT[:, g * 128:(g + 1) * 128], Wpool[:, g * 128:(g + 1) * 128],
                           g == 0, g == 3, (bf("diffT"), bW), (bank[4],))
                for h in range(8):
                    b = sbs[h // 4]
                    o = PB(b, (h % 4) * 128, (h % 4 + 1) * 128)
                    stt("dve", o[:, :], A[:, :], -8.0 * slopes[h], o[:, :], ALU.mult, ALU.add, (bT, bank[b]), (bank[b],))
                for g in range(2):
                    act(PT[kt_i][:, g * 512:(g + 1) * 512], PB(sbs[g])[:, :], AF.Exp, (bank[sbs[g]],), (*PTB[kt_i],), scale=0.125)

            s_block(0, KT0[prv], bf("KT0%s" % "ab"[prv]), Aprev)
            tt("dve", tmpE[:, 0:512], PB(4)[:, :], PSC, ALU.mult, (bank[4], bT), (bE,))
            s_block(1, KT0[cur], bkt, Aown)
            obs = (4, 5)
            for h in range(8):
                b = obs[h // 4]
                o = PB(b, (h % 4) * 65, (h % 4) * 65 + 65)
                mm(o[:, :], PT[0][:, h * 128:(h + 1) * 128], vp[:, (h // 4) * 65:(h // 4) * 65 + 65],
                   h % 4 == 0, False, (*PTB[0], bvp), (bank[b],))
                mm(o[:, :], PT[1][:, h * 128:(h + 1) * 128], vc[:, (h // 4) * 65:(h // 4) * 65 + 65],
                   False, True, (*PTB[1], bvc), (bank[b],))
            for half in range(2):
                b = obs[half]
                O3 = PB(b, 0, 260).rearrange("p (h d) -> p h d", h=4)
                tt("dve", sm[:, 32 + half * 4: 36 + half * 4].unsqueeze(2), O3[:, :, 64:65],
                   Esink[:, half * 4:(half + 1) * 4].unsqueeze(2), ALU.add, (bank[b], bf("esink")), (bf("sm_l0"),))
                recip(sm[:, 32 + half * 4: 36 + half * 4], sm[:, 32 + half * 4: 36 + half * 4], (bf("sm_l0"),), (bf("sm_l0"),))
                tt("dve", v3(tmpE[:, 512 + half * 256: 768 + half * 256], 4), O3[:, :, 0:64],
                   sm[:, 32 + half * 4: 36 + half * 4].unsqueeze(2).to_broadcast([128, 4, 64]), ALU.mult,
                   (bank[b], bf("sm_l0")), (bE,))
            tt("dve", Mx[:, :], Mx[:, :], tmpE[:, :], ALU.mult, (bMx, bE), (bMx,))
            out_proj(Mx[:, :], bMx, W0out, bW, X, bxin, x1b[cur], bx1, bt=3, bo=(4, 5), T1=T3, t1n="QT")

        def L1a(v):
            owned = (v % 8 == 7)
            slot = v // 8
            xin1, bx1 = x1b[v % 2], bf("x1_%d" % (v % 2))
            rms_to_T(xin1[:, :], bx1, T2, bf("T2"), xs_t=xsB, xs_n="xsB", smc=128, bt=6)
            proj(T2, bf("T2"), W1kv, 2064, 0, 512, 6)
            proj(T2, bf("T2"), W1kv, 2064, 512, 512, 7)
            blg = bf("logf")
            head_norm(PB(6)[:, :], (bank[6],), 8, KN1, tmpC[:, 0:512], bC, 136, scr=tmpD, scr_b=bDt)
            proj(T2, bf("T2"), W1kv, 2064, 2048, 16, 6)
            head_norm(PB(7)[:, :], (bank[7],), 8, KN1, tmpC[:, 512:1024], bC, 144, scr=tmpD, scr_b=bDt)
            tt("dve", logf[:, :], PB(6, 0, 16)[:, :], BFG, ALU.add, (bank[6], bT), (blg,))
            act(logf[:, :], logf[:, :], AF.Exp, (blg,), (blg,), scale=-1.0)
            act(logf[:, :], logf[:, :], AF.Ln, (blg, bT), (blg,), bias=ONEC[:, :])
            ts("dve", logf[:, :], logf[:, :], -1.0, None, ALU.mult, None, (blg,), (blg,))
            proj(T2, bf("T2"), W1kv, 2064, 1536, 512, 7)
            mm(PB(6, 16, 32)[:, :], Tri[:, :], logf[:, :], True, True, (bT, blg), (bank[6],))
            mm(PB(6, 32, 48)[:, :], Ones[:, :], logf[:, :], True, True, (bT, blg), (bank[6],))
            tt("dve", cum[:, :], PB(6, 16, 32)[:, :], carry[:, :], ALU.add, (bank[6], bf("carry")), (bf("cum"),))
            tt("dve", carry[:, :], PB(6, 32, 48)[:, :], carry[:, :], ALU.add, (bank[6], bf("carry")), (bf("carry"),))
            proj(T2, bf("T2"), W1kv, 2064, 1024, 512, 6)
            bc8 = bf("c8")
            ts("dve", c8[:, 0:16], cum[:, :], 8.0, None, ALU.mult, None, (bf("cum"),), (bc8,))
            cp("dve", chi[:, 0:16], c8[:, 0:16], (bc8,), (bf("chi"),))
            tt("dve", c8[:, 16:32], c8[:, 0:16], chi[:, 0:16], ALU.subtract, (bc8, bf("chi")), (bc8,))
            cp("dve", chi[:, 16:32], c8[:, 16:32], (bc8,), (bf("chi"),))
            tt("dve", c8[:, 32:48], c8[:, 16:32], chi[:, 16:32], ALU.subtract, (bc8, bf("chi")), (bc8,))
            cp("dve", chi[:, 32:48], c8[:, 32:48], (bc8,), (bf("chi"),))
            K3 = v3(Kaug[:, :], 16)
            cp("dve", K3[:, :, 0:64], v3(tmpC[:, 0:1024], 16), (bC,), (bK,))
            for i in range(3):
                ts("dve", K3[:, :, 67 + i:68 + i], chi[:, 16 * i:16 * i + 16].unsqueeze(2), -1.0, None, ALU.mult, None,
                   (bf("chi"),), (bK,))
            ts("dve", K3[:, :, 70:71], Ones[:, 0:16].unsqueeze(2), Kval[:, v:v + 1], None, ALU.mult, None, (bT,), (bK,))
            V3 = v3(Vaug[:, :], 16)
            cp("act", V3[:, 0:8, 0:64], v3(PB(6)[:, :], 8), (bank[6],), (bV,))
            cp("act", V3[:, 8:16, 0:64], v3(PB(7)[:, :], 8), (bank[7],), (bV,))
            if owned:
                dma(fk_o[slot, :, :], tmpC[:, 0:1024], (bC,), (bf("fk_o"),))
                cp("dve", tmpD[:, 0:512], PB(6)[:, :], (bank[6],), (bDt,))
                cp("dve", tmpD[:, 512:1024], PB(7)[:, :], (bank[7],), (bDt,))
                dma(fv_o[slot, :, :], tmpD[:, :], (bDt,), (bf("fv_o"),))
                dma(fl_o[slot, :, :], logf[:, :], (blg,), (bf("fl_o"),))
            for h in range(16):
                b = 6 + h // 8
                tr(PBb(b)[0:KA, (h % 8) * 128:(h % 8 + 1) * 128], Kaug[:, h * KA:(h + 1) * KA], (bK, bT), (bank[b],))
            cp("act", KTo[0:KA, 0:1024], PBb(6)[0:KA, :], (bank[6],), (bf("KTo"),))
            cp("dve", KTo[0:KA, 1024:2048], PBb(7)[0:KA, :], (bank[7],), (bf("KTo"),))
            bKTd, bVd = bf("KT_d%d" % v), bf("V_d%d" % v)
            dma(KT_d[v, :, :], KTo[0:KTP, :], (bf("KTo"),), (bKTd,))
            dma(V_d[v, :, :], Vaug[:, :], (bV,), (bVd,))

        L0a(0)
        L0b(0)
        if NT > 1:
            L0a(1)
        for v in range(NT):
            owned = (v % 8 == 7)
            slot = v // 8
            FILL[0] = 2
            sA = P.captured(lambda: L0a(v + 2)) if v + 2 < NT else []
            sB = P.captured(lambda: L0b(v + 1)) if v + 1 < NT else []
            sC = P.captured(lambda: L1a(v))
            FILL[0] = 0
            for sp in Prog.interleave(Prog.interleave(sA, sB), sC):
                P.commit(*sp)
            x1 = x1b[v % 2]
            bx1 = bf("x1_%d" % (v % 2))
            if not owned:
                continue
            dma(Wst[:, :], W_d[0, :, :], (bWd,), (bWst,))
            proj(T2, bf("T2"), Wst, 1024, 0, 512, 1, wbuf=bWst)
            proj(T2, bf("T2"), Wst, 1024, 512, 512, 2, wbuf=bWst)
            head_norm(PB(1)[:, :], (bank[1],), 8, QN1, tmpA[:, 0:512], bA, 8)
            head_norm(PB(2)[:, :], (bank[2],), 8, QN1, tmpA[:, 512:1024], bA, 16)
            Q3 = v3(Qaug[:, :], 16)
            cp("act", Q3[:, :, 0:64], v3(tmpA[:, :], 16), (bA,), (bQ,))
            for i in range(3):
                cp("dve", Q3[:, :, 64 + i:65 + i], chi[:, 16 * i:16 * i + 16].unsqueeze(2), (bf("chi"),), (bQ,))
            for h in range(16):
                b = 6 + h // 8
                tr(PBb(b)[0:KA, (h % 8) * 128:(h % 8 + 1) * 128], Qaug[:, h * KA:(h + 1) * KA], (bQ, bT), (bank[b],))
            cp("act", QT[0:KA, 0:1024], PBb(6)[0:KA, :], (bank[6],), (bf("QT"),))
            cp("dve", QT[0:KA, 1024:2048], PBb(7)[0:KA, :], (bank[7],), (bf("QT"),))
            dma(Wst[:, :], W_d[1, :, :], (bWd,), (bWst,))
            proj(T2, bf("T2"), Wst, 1024, 0, 512, 1, wbuf=bWst)
            proj(T2, bf("T2"), Wst, 1024, 512, 512, 2, wbuf=bWst)
            silu_gate((1, 2), tmpE, bf("tmpE"))
            dma(Wst[:, :], W_d[2, :, :], (bWd,), (bWst,))
            if KSTOP <= 6:
                continue
            def obank(h):
                return (6, 7, 1)[h // 7], (h % 7) * 65
            LAG = 2
            NR = len(KTr)
            units = [(kt, qd) for kt in range(v + 1) for qd in range(4)]

            def emit_pv(u):
                kt, qd = units[u]
                r_ = kt % NR
                pq = u % 4
                ptb = PT[pq // 2][:, (pq % 2) * 512:(pq % 2 + 1) * 512]
                for hh in range(4):
                    h = qd * 4 + hh
                    ob, oc = obank(h)
                    mm(PB(ob, oc, oc + 65)[:, :], ptb[:, hh * 128:(hh + 1) * 128], Vr[r_][:, h * 65:(h + 1) * 65],
                       kt == 0 and h in (0, 7, 14), kt == v, (bf("PTq%d" % pq), bf("Vr%d" % r_)), (bank[ob],))

            for u, (kt, qd) in enumerate(units):
                r_ = kt % NR
                bktr, bvr = bf("KTr%d" % r_), bf("Vr%d" % r_)
                if qd == 0:
                    dma(KTr[r_][0:KTP, :], KT_d[kt, :, :], (bf("KT_d%d" % kt),), (bktr,))
                    dma(Vr[r_][:, :], V_d[kt, :, :], (bf("V_d%d" % kt),), (bvr,))
                b = 2 + u % 4
                pq = u % 4
                for hh in range(4):
                    h = qd * 4 + hh
                    mm(PB(b, hh * 128, (hh + 1) * 128)[:, :], KTr[r_][0:KA, h * 128:(h + 1) * 128],
                       QT[0:KA, h * 128:(h + 1) * 128], hh == 0, hh == 3, (bktr, bf("QT")), (bank[b],))
                if kt == v:
                    tt("dve", PB(b)[:, :], PB(b)[:, :], Mc[:, :], ALU.add, (bank[b], bT), (bank[b],))
                act(PT[pq // 2][:, (pq % 2) * 512:(pq % 2 + 1) * 512], PB(b)[:, :], AF.Exp, (bank[b],), (bf("PTq%d" % pq),), scale=0.125)
                if u >= LAG:
                    emit_pv(u - LAG)
                mm(PB(0)[:, :], Ident[:, :], W0out[:, 0:512], True, True, (bT, bW), (bank[0],))
            for u in range(max(0, len(units) - LAG), len(units)):
                emit_pv(u)
            bOe = bf("tmpC")
            for gi, (ob, nh) in enumerate(((6, 7), (7, 7), (1, 2))):
                cp("act" if gi != 1 else "dve", Oe[:, gi * 455: gi * 455 + nh * 65], PB(ob, 0, nh * 65)[:, :], (bank[ob],), (bOe,))
            O3 = v3(Oe[:, :], 16)
            recip(sm[:, 64:80].unsqueeze(2), O3[:, :, 64:65], (bOe,), (bf("sm_l1"),))
            tt("dve", v3(tmpA[:, :], 16), O3[:, :, 0:64], sm[:, 64:80].unsqueeze(2).to_broadcast([128, 16, 64]), ALU.mult,
               (bOe, bf("sm_l1")), (bA,))
            tt("dve", tmpE[:, :], tmpE[:, :], tmpA[:, :], ALU.mult, (bf("tmpE"), bA), (bf("tmpE"),))
            out_proj(tmpE[:, :], bf("tmpE"), Wst, bWst, x1, bx1, tmpB, bBt)
            dma(y_o[slot, :, :], tmpB[:, :], (bBt,), (bf("y_o"),))

        NQ = 32
        PAST = 4096
        x1, bx1 = x1b[0], bf("x1_0")
        for sbi in range(2):
            bx = bf("xin%d" % (sbi % 2))
            X = xin[sbi % 2]
            dma(Dcur[:, :], dsel_d[8, :, :], (), (bf("Dcur"),))
            dma(X[0:NQ, :], xs_d[sbi, :, :], (), (bx,))
            rms_to_T(X[0:NQ, :], bx, T1, bf("T1"), n=NQ)
            proj(T1, bf("T1"), W0in, 2304, 0, 512, 1, n=NQ)
            proj(T1, bf("T1"), W0in, 2304, 512, 512, 2, n=NQ)
            proj(T1, bf("T1"), W0in, 2304, 1024, 256, 3, n=NQ)
            proj(T1, bf("T1"), W0in, 2304, 1280, 512, 4, n=NQ)
            proj(T1, bf("T1"), W0in, 2304, 1792, 512, 5, n=NQ)
            memset("pool", tmpA[:, 0:512], 0.0, (bA,))
            dma(tmpA[113:128, 0:512], spool_d[sbi, :, :], (), (bA,))
            cp("dve", ubf[1][:, :], tmpA[:, 0:512], (bA,), (bf("ubf1"),))
            cp("act", ubf[0][0:NQ, :], PB(1)[0:NQ, :], (bank[1],), (bf("ubf0"),))
            cp("dve", tmpB[0:NQ, 0:512], PB(1)[0:NQ, :], (bank[1],), (bBt,))
            dma(pools_o[sbi, :, :], tmpB[17:32, 0:512], (bBt,), (bf("pools_o"),))
            dma(tmpB[:, 512:640], cswk_d[sbi, :, :], (), (bBt,))
            dma(tmpB[:, 640:768], cswv_d[sbi, :, :], (), (bBt,))
            cp("act", v3(k0aug[:, :], 2)[:, :, 0:64], v3(tmpB[:, 512:640], 2), (bBt,), (bk0,))
            ts("dve", v3(k0aug[:, :], 2)[:, :, 64:65], Ones[:, 0:2].unsqueeze(2), 0.0, None, ALU.mult, None, (bT,), (bk0,))
            cp("dve", v3(v0aug[0][:, :], 2)[:, :, 0:64], v3(tmpB[:, 640:768], 2), (bBt,), (bf("v0aug0"),))
            for g in range(2):
                tr(PBb(6)[0:65, g * 128:(g + 1) * 128], k0aug[:, g * 65:(g + 1) * 65], (bk0, bT), (bank[6],))
            cp("dve", KT0[0][0:65, :], PBb(6)[0:65, 0:256], (bank[6],), (bf("KT0a"),))
            dma(swks_o[sbi, 0:96, :], cswk_d[sbi, 32:128, :], (), (bf("swks_o"),))
            dma(swvs_o[sbi, 0:96, :], cswv_d[sbi, 32:128, :], (), (bf("swvs_o"),))
            head_norm(PB(2)[0:NQ, :], (bank[2],), 8, QN0, tmpA[0:NQ, 0:512], bA, 8, n=NQ)
            cp("act", v3(q0aug[0:NQ, :], 8)[:, :, 0:64], v3(tmpA[0:NQ, 0:512], 8), (bA,), (bq0,))
            head_norm(PB(3, 0, 128)[0:NQ, :], (bank[3],), 2, KN0, tmpA[0:NQ, 512:640], bA, 16, n=NQ)
            cp("dve", v3(k0aug[0:NQ, :], 2)[:, :, 0:64], v3(tmpA[0:NQ, 512:640], 2), (bA,), (bk0,))
            dma(swks_o[sbi, 96:128, :], tmpA[0:NQ, 512:640], (bA,), (bf("swks_o"),))
            cp("act", v3(v0aug[1][0:NQ, :], 2)[:, :, 0:64], v3(PB(3, 128, 256)[0:NQ, :], 2), (bank[3],), (bf("v0aug1"),))
            cp("dve", tmpA[0:NQ, 640:768], PB(3, 128, 256)[0:NQ, :], (bank[3],), (bA,))
            dma(swvs_o[sbi, 96:128, :], tmpA[0:NQ, 640:768], (bA,), (bf("swvs_o"),))
            silu_gate((4, 5), MIX, bMIX, n=NQ)
            for g in range(4):
                mm(PB(6, g * 128, g * 128 + NQ)[:, :], ubf[0][0:NQ, g * 128:(g + 1) * 128], Dcur[0:NQ, g * 128:g * 128 + NQ],
                   g == 0, False, (bf("ubf0"), bf("Dcur")), (bank[6],))
                mm(PB(6, g * 128, g * 128 + NQ)[:, :], ubf[1][:, g * 128:(g + 1) * 128], Dprev[:, g * 128:g * 128 + NQ],
                   False, True, (bf("ubf1"), bT), (bank[6],))
            cp("act", v3(diffT[:, :], 4)[:, :, 0:NQ], v3(PB(6)[:, :], 4)[:, :, 0:NQ], (bank[6],), (bf("diffT"),))
            for g in range(4):
                mm(PB(7, g * 128, (g + 1) * 128)[0:NQ, :], diffT[:, g * 128:g * 128 + NQ], Wpool[:, g * 128:(g + 1) * 128],
                   g == 0, g == 3, (bf("diffT"), bW), (bank[7],))
            for h in range(8):
                tr(PBb(0)[0:65, h * 128:h * 128 + NQ], q0aug[0:NQ, h * 65:(h + 1) * 65], (bq0, bT), (bank[0],))
            cp("act", v3(QT[0:65, 0:1024], 8)[:, :, 0:NQ], v3(PBb(0)[0:65, :], 8)[:, :, 0:NQ], (bank[0],), (bf("QT"),))
            for g in range(2):
                tr(PBb(6)[0:65, g * 128:g * 128 + NQ], k0aug[0:NQ, g * 65:(g + 1) * 65], (bk0, bT), (bank[6],))
            cp("dve", v3(KT0[1][0:65, :], 2)[:, :, 0:NQ], v3(PBb(6)[0:65, 0:256], 2)[:, :, 0:NQ], (bank[6],), (bf("KT0b"),))
            for h in range(8):
                g = h // 4
                mm(PB(1, h * NQ, (h + 1) * NQ)[:, :], KT0[0][0:65, g * 128:(g + 1) * 128], QT[0:65, h * 128:h * 128 + NQ],
                   h == 0, h == 7, (bf("KT0a"), bf("QT")), (bank[1],))
                mm(PB(2, h * NQ, (h + 1) * NQ)[0:NQ, :], KT0[1][0:65, g * 128:g * 128 + NQ], QT[0:65, h * 128:h * 128 + NQ],
                   h == 0, h == 7, (bf("KT0b"), bf("QT")), (bank[2],))
            for h in range(8):
                o = PB(1, h * NQ, (h + 1) * NQ)
                stt("dve", o[:, :], Aprev[:, 0:NQ], -8.0 * slopes[h], o[:, :], ALU.mult, ALU.add, (bT, bank[1]), (bank[1],))
                o = PB(2, h * NQ, (h + 1) * NQ)
                stt("dve", o[0:NQ, :], Aown[0:NQ, 0:NQ], -8.0 * slopes[h], o[0:NQ, :], ALU.mult, ALU.add, (bT, bank[2]), (bank[2],))
            act(PT[0][:, 0:256], PB(1, 0, 256)[:, :], AF.Exp, (bank[1],), (*PTB[0],), scale=0.125)
            act(PT[1][0:NQ, 0:256], PB(2, 0, 256)[0:NQ, :], AF.Exp, (bank[2],), (*PTB[1],), scale=0.125)
            tt("dve", tmpA[0:NQ, 0:512], PB(7)[0:NQ, :], PSC[0:NQ, :], ALU.mult, (bank[7], bT), (bA,))
            for h in range(8):
                b_ = 5 + h // 4
                o = PB(b_, (h % 4) * 65, (h % 4) * 65 + 65)
                g = h // 4
                mm(o[0:NQ, :], PT[0][:, h * NQ:(h + 1) * NQ], v0aug[0][:, g * 65:g * 65 + 65],
                   h % 4 == 0, False, (*PTB[0], bf("v0aug0")), (bank[b_],))
                mm(o[0:NQ, :], PT[1][0:NQ, h * NQ:(h + 1) * NQ], v0aug[1][0:NQ, g * 65:g * 65 + 65],
                   False, True, (*PTB[1], bf("v0aug1")), (bank[b_],))
            for half in range(2):
                b_ = 5 + half
                O3 = PB(b_, 0, 260)[0:NQ, :].rearrange("p (h d) -> p h d", h=4)
                tt("dve", sm[0:NQ, 32 + half * 4: 36 + half * 4].unsqueeze(2), O3[:, :, 64:65],
                   Esink[0:NQ, half * 4:(half + 1) * 4].unsqueeze(2), ALU.add, (bank[b_], bf("esink")), (bf("sm_l0"),))
                recip(sm[0:NQ, 32 + half * 4: 36 + half * 4], sm[0:NQ, 32 + half * 4: 36 + half * 4], (bf("sm_l0"),), (bf("sm_l0"),))
                for hh in range(4):
                    h = half * 4 + hh
                    ts("dve", tmpA[0:NQ, 512 + h * 64: 576 + h * 64], PB(b_, hh * 65, hh * 65 + 64)[0:NQ, :], sm[0:NQ, 32 + h:33 + h], None,
                       ALU.mult, None, (bank[b_], bf("sm_l0")), (bA,))
            tt("dve", MIX[0:NQ, :], MIX[0:NQ, :], tmpA[0:NQ, :], ALU.mult, (bMIX, bA), (bMIX,))
            out_proj(MIX[0:NQ, :], bMIX, W0out, bW, X, bx, x1, bx1, n=NQ)

            rms_to_T(x1[0:NQ, :], bx1, T2, bf("T2"), n=NQ)
            proj(T2, bf("T2"), W1kv, 2064, 0, 512, 1, n=NQ)
            proj(T2, bf("T2"), W1kv, 2064, 512, 512, 2, n=NQ)
            proj(T2, bf("T2"), W1kv, 2064, 1024, 512, 3, n=NQ)
            proj(T2, bf("T2"), W1kv, 2064, 1536, 512, 4, n=NQ)
            proj(T2, bf("T2"), W1kv, 2064, 2048, 16, 5, n=NQ)
            blg = bf("logf")
            tt("dve", logf[0:NQ, :], PB(5, 0, 16)[0:NQ, :], BFG[0:NQ, :], ALU.add, (bank[5], bT), (blg,))
            act(logf[0:NQ, :], logf[0:NQ, :], AF.Exp, (blg,), (blg,), scale=-1.0)
            act(logf[0:NQ, :], logf[0:NQ, :], AF.Ln, (blg, bT), (blg,), bias=ONEC[0:NQ, :])
            ts("dve", logf[0:NQ, :], logf[0:NQ, :], -1.0, None, ALU.mult, None, (blg,), (blg,))
            dma(fls_o[sbi, :, :], logf[0:NQ, :], (blg,), (bf("fls_o"),))
            cp("act", logfn[0:NQ, :], logf[0:NQ, :], (blg,), (bf("logfn"),))
            head_norm(PB(1)[0:NQ, :], (bank[1],), 8, KN1, tmpA[0:NQ, 0:512], bA, 8, n=NQ)
            head_norm(PB(2)[0:NQ, :], (bank[2],), 8, KN1, tmpA[0:NQ, 512:1024], bA, 16, n=NQ)
            dma(fks_o[sbi, :, :], tmpA[0:NQ, :], (bA,), (bf("fks_o"),))
            cp("dve", Gb[0:NQ, :], tmpA[0:NQ, :], (bA,), (bf("Gb"),))
            cp("dve", tmpB[0:NQ, 0:512], PB(3)[0:NQ, :], (bank[3],), (bBt,))
            cp("dve", tmpB[0:NQ, 512:1024], PB(4)[0:NQ, :], (bank[4],), (bBt,))
            dma(fvs_o[sbi, :, :], tmpB[0:NQ, :], (bBt,), (bf("fvs_o"),))
            cp("act", xs[0:NQ, :], tmpB[0:NQ, :], (bBt,), (bf("xs"),))
            dma(Wst[:, :], W_d[0, :, :], (bWd,), (bWst,))
            proj(T2, bf("T2"), Wst, 1024, 0, 512, 1, n=NQ, wbuf=bWst)
            proj(T2, bf("T2"), Wst, 1024, 512, 512, 2, n=NQ, wbuf=bWst)
            head_norm(PB(1)[0:NQ, :], (bank[1],), 8, QN1, tmpA[0:NQ, 0:512], bA, 8, n=NQ)
            head_norm(PB(2)[0:NQ, :], (bank[2],), 8, QN1, tmpA[0:NQ, 512:1024], bA, 16, n=NQ)
            Q3 = v3(Qaug[0:NQ, :], 16)
            cp("dve", Q3[:, :, 0:64], v3(tmpA[0:NQ, :], 16), (bA,), (bQ,))
            dma(Wst[:, :], W_d[1, :, :], (bWd,), (bWst,))
            proj(T2, bf("T2"), Wst, 1024, 0, 512, 1, n=NQ, wbuf=bWst)
            proj(T2, bf("T2"), Wst, 1024, 512, 512, 2, n=NQ, wbuf=bWst)
            silu_gate((1, 2), MIX, bMIX, n=NQ)
            dma(Wst[:, :], W_d[2, :, :], (bWd,), (bWst,))
            memset("dve", carry[:, :], 0.0, (bf("carry"),))
            K3 = v3(Kaug[:, :], 16)
            V3 = v3(Vaug[:, :], 16)
            ts("dve", K3[:, :, 70:71], Ones[:, 0:16].unsqueeze(2), 0.0, None, ALU.mult, None, (bT,), (bK,))
            bc8 = bf("c8")
            bOe = bf("tmpC")

            def obank(h):
                return (6, 7, 1)[h // 7], (h % 7) * 65

            def split_cum(nrow):
                ts("dve", c8[0:nrow, 0:16], cum[0:nrow, :], 8.0, None, ALU.mult, None, (bf("cum"),), (bc8,))
                cp("dve", chi[0:nrow, 0:16], c8[0:nrow, 0:16], (bc8,), (bf("chi"),))
                tt("dve", c8[0:nrow, 16:32], c8[0:nrow, 0:16], chi[0:nrow, 0:16], ALU.subtract, (bc8, bf("chi")), (bc8,))
                cp("dve", chi[0:nrow, 16:32], c8[0:nrow, 16:32], (bc8,), (bf("chi"),))
                tt("dve", c8[0:nrow, 32:48], c8[0:nrow, 16:32], chi[0:nrow, 16:32], ALU.subtract, (bc8, bf("chi")), (bc8,))
                cp("dve", chi[0:nrow, 32:48], c8[0:nrow, 32:48], (bc8,), (bf("chi"),))

            NKT = PAST // 128
            order = [None] + list(range(NKT - 1, -1, -1))
            for it, kt in enumerate(order):
                new = kt is None
                first, lastit = (it == 0), (it == len(order) - 1)
                nk = NQ if new else 128
                if new:
                    cp("act", K3[0:NQ, :, 0:64], v3(Gb[0:NQ, :], 16), (bf("Gb"),), (bK,))
                    cp("dve", V3[0:NQ, :, 0:64], v3(xs[0:NQ, :], 16), (bf("xs"),), (bV,))
                    mm(PB(5, 16, 32)[0:NQ, :], Tri[0:NQ, 0:NQ], logfn[0:NQ, :], True, True, (bT, bf("logfn")), (bank[5],))
                    cp("dve", cum[0:NQ, :], PB(5, 16, 32)[0:NQ, :], (bank[5],), (bf("cum"),))
                else:
                    kst, kstb = xin[it % 2], bf("xin%d" % (it % 2))
                    vst, vstb = (tmpA, bA) if it % 2 == 0 else (tmpB, bBt)
                    dma(kst[:, :], cfk_d[sbi, kt * 128:(kt + 1) * 128, :], (), (kstb,))
                    dma(vst[:, :], cfv_d[sbi, kt * 128:(kt + 1) * 128, :], (), (vstb,))
                    dma(logf[:, :], cfl_d[sbi, kt * 128:(kt + 1) * 128, :], (), (blg,))
                    cp("dve" if it % 2 else "act", K3[:, :, 0:64], v3(kst[:, :], 16), (kstb,), (bK,))
                    cp("act" if it % 2 else "dve", V3[:, :, 0:64], v3(vst[:, :], 16), (vstb,), (bV,))
                    mm(PB(5, 16, 32)[:, :], Tri[:, :], logf[:, :], True, True, (bT, blg), (bank[5],))
                    mm(PB(5, 32, 48)[:, :], Ones[:, :], logf[:, :], True, True, (bT, blg), (bank[5],))
                    tt("dve", cum[:, :], PB(5, 16, 32)[:, :], carry[:, :], ALU.subtract, (bank[5], bf("carry")), (bf("cum"),))
                    tt("dve", cum[:, :], cum[:, :], PB(5, 32, 48)[:, :], ALU.subtract, (bf("cum"), bank[5]), (bf("cum"),))
                    tt("dve", carry[:, :], PB(5, 32, 48)[:, :], carry[:, :], ALU.add, (bank[5], bf("carry")), (bf("carry"),))
                split_cum(nk)
                for i in range(3):
                    ts("dve", K3[0:nk, :, 67 + i:68 + i], chi[0:nk, 16 * i:16 * i + 16].unsqueeze(2), -1.0, None, ALU.mult, None,
                       (bf("chi"),), (bK,))
                if new:
                    for i in range(3):
                        cp("dve", Q3[:, :, 64 + i:65 + i], chi[0:NQ, 16 * i:16 * i + 16].unsqueeze(2), (bf("chi"),), (bQ,))
                    for h in range(16):
                        b_ = 2 + h // 8
                        tr(PBb(b_)[0:KA, (h % 8) * 128:(h % 8) * 128 + NQ], Qaug[0:NQ, h * KA:(h + 1) * KA], (bQ, bT), (bank[b_],))
                    cp("act", v3(QT[0:KA, 0:1024], 8)[:, :, 0:NQ], v3(PBb(2)[0:KA, :], 8)[:, :, 0:NQ], (bank[2],), (bf("QT"),))
                    cp("dve", v3(QT[0:KA, 1024:2048], 8)[:, :, 0:NQ], v3(PBb(3)[0:KA, :], 8)[:, :, 0:NQ], (bank[3],), (bf("QT"),))
                for h in range(16):
                    b_ = 2 + h // 8
                    tr(PBb(b_)[0:KA, (h % 8) * 128:(h % 8) * 128 + nk], Kaug[0:nk, h * KA:(h + 1) * KA], (bK, bT), (bank[b_],))
                ktr = KTr[it % 2]
                bktr = bf("KTr%d" % (it % 2))
                cp("act", v3(ktr[0:KA, 0:1024], 8)[:, :, 0:nk], v3(PBb(2)[0:KA, :], 8)[:, :, 0:nk], (bank[2],), (bktr,))
                cp("dve", v3(ktr[0:KA, 1024:2048], 8)[:, :, 0:nk], v3(PBb(3)[0:KA, :], 8)[:, :, 0:nk], (bank[3],), (bktr,))
                sbk = (0, 4)[it % 2]
                for h in range(16):
                    mm(PB(sbk, h * NQ, (h + 1) * NQ)[0:nk, :], ktr[0:KA, h * 128:h * 128 + nk], QT[0:KA, h * 128:h * 128 + NQ],
                       h == 0, h == 15, (bktr, bf("QT")), (bank[sbk],))
                if new:
                    for h in range(16):
                        o = PB(sbk, h * NQ, (h + 1) * NQ)
                        tt("dve", o[0:NQ, :], o[0:NQ, :], Mc[0:NQ, 0:NQ], ALU.add, (bank[sbk], bT), (bank[sbk],))
                pt = PT[it % 2]
                bpt = PTB[it % 2]
                act(pt[0:nk, 0:512], PB(sbk)[0:nk, :], AF.Exp, (bank[sbk],), bpt, scale=0.125)
                for h in range(16):
                    ob, oc = obank(h)
                    mm(PB(ob, oc, oc + 65)[0:NQ, :], pt[0:nk, h * NQ:(h + 1) * NQ], Vaug[0:nk, h * 65:(h + 1) * 65],
                       first and h in (0, 7, 14), lastit, bpt + (bV,), (bank[ob],))
            for gi, (ob, nh) in enumerate(((6, 7), (7, 7), (1, 2))):
                cp("act" if gi != 1 else "dve", Oe[0:NQ, gi * 455: gi * 455 + nh * 65], PB(ob, 0, nh * 65)[0:NQ, :], (bank[ob],), (bOe,))
            O3 = v3(Oe[0:NQ, :], 16)
            recip(sm[0:NQ, 64:80].unsqueeze(2), O3[:, :, 64:65], (bOe,), (bf("sm_l1"),))
            for h in range(16):
                ts("dve", tmpA[0:NQ, h * 64:(h + 1) * 64], Oe[0:NQ, h * 65:h * 65 + 64], sm[0:NQ, 64 + h:65 + h], None,
                   ALU.mult, None, (bOe, bf("sm_l1")), (bA,))
            tt("dve", MIX[0:NQ, :], MIX[0:NQ, :], tmpA[0:NQ, :], ALU.mult, (bMIX, bA), (bMIX,))
            out_proj(MIX[0:NQ, :], bMIX, Wst, bWst, x1, bx1, tmpB, bBt, n=NQ)
            dma(ys_o[sbi, :, :], tmpB[0:NQ, :], (bBt,), (bf("ys_o"),))

        P.finalize()
        P.emit(nc, block, esem, dsem)
    return nc


def _tables():
    t = np.arange(128)
    W = (2, 4, 8, 16)
    dgen = np.zeros((128, 4, 128), np.float32)
    dfirst = np.zeros((128, 4, 128), np.float32)
    dprev = np.zeros((128, 4, 128), np.float32)
    for g, w in enumerate(W):
        inwin = (t[:, None] <= t[None, :]) & (t[:, None] > t[None, :] - w)
        dgen[:, g, :] = inwin / float(w) - np.eye(128)
        dfirst[:, g, :] = inwin / np.minimum(t[None, :] + 1, w).astype(np.float32) - np.eye(128)
        dist = 128 + t[None, :] - t[:, None]
        dprev[:, g, :] = (dist < w) / float(w)
    BIG = 1e7
    tq, kj = t[None, :], t[:, None]
    cq, ck = tq // 64, kj // 64
    aprev = np.abs(128 + tq - kj).astype(np.float32) + BIG * ((cq == 1) & (ck == 0))
    aown = np.abs(tq - kj).astype(np.float32) + BIG * ((cq == 0) & (ck == 1))
    mc = np.where(kj > tq, NEG, 0.0).astype(np.float32)
    tri = (kj <= tq).astype(np.float32)
    return dgen, dfirst, dprev, aprev.astype(np.float32), aown.astype(np.float32), mc, tri


def kernel(x_prompt, x_sample, state_pool, cache_swa_k, cache_swa_v, cache_fox_k, cache_fox_v, cache_fox_logf,
           norm0_g, w_in0, w_pool, pool_scale, swa_qn_g, swa_kn_g, swa_sinks, w_out0,
           norm1_g, w_in1, b_forget, fox_qn_g, fox_kn_g, w_out1, _nt=None, _raw=False):
    NT = NT_FULL if _nt is None else _nt
    f32 = lambda a: np.ascontiguousarray(np.asarray(a), dtype=np.float32)
    bf = ml_dtypes.bfloat16
    xpr = f32(x_prompt)[0]
    dgen, dfirst, dprev, aprev, aown, mc, tri = _tables()
    vecs = np.zeros((1, 1024), np.float32)
    vecs[0, 0:512] = f32(pool_scale)
    vecs[0, 512:576] = f32(swa_qn_g); vecs[0, 576:640] = f32(swa_kn_g)
    vecs[0, 640:648] = f32(swa_sinks)
    vecs[0, 656:672] = f32(b_forget)
    vecs[0, 672:736] = f32(fox_qn_g); vecs[0, 736:800] = f32(fox_kn_g)
    common = {
        "dprev": dprev.reshape(128, 512).astype(bf), "aprev": aprev, "aown": aown, "mc": np.ascontiguousarray(np.tile(mc, (1, 4))), "tri": tri,
        "ones": np.ones((128, 128), np.float32), "ident": np.eye(128, dtype=np.float32).astype(bf),
        "w_in0": f32(w_in0), "w_out0": f32(w_out0), "w_in1": f32(w_in1), "w_out1": f32(w_out1), "w_pool": f32(w_pool),
        "g0c": np.ascontiguousarray(f32(norm0_g).reshape(8, 128).T), "g1c": np.ascontiguousarray(f32(norm1_g).reshape(8, 128).T),
        "vecs": vecs,
    }
    in_maps = []
    for c in range(NCORES):
        pad = 7 - c
        nreal = NT - pad
        xp = np.zeros((NT * 128, D), np.float32)
        xp[pad * 128:] = xpr[: nreal * 128]
        kval = np.zeros((128, NT), np.float32)
        kval[:, :pad] = NEG
        dsel = np.broadcast_to(dgen.reshape(1, 128, 512), (9, 128, 512)).copy()
        dsel[pad] = dfirst.reshape(128, 512)
        m = dict(common)
        m.update({"xp": xp, "kval": kval, "dsel": dsel.astype(bf)})
        sl = slice(2 * c, 2 * c + 2)
        m.update({
            "xs_in": f32(x_sample)[sl], "spool": f32(state_pool)[sl],
            "cswk": f32(cache_swa_k)[sl].reshape(2, 128, 128), "cswv": f32(cache_swa_v)[sl].reshape(2, 128, 128),
            "cfk": f32(cache_fox_k)[sl].reshape(2, 4096, D), "cfv": f32(cache_fox_v)[sl].reshape(2, 4096, D),
            "cfl": f32(cache_fox_logf)[sl],
        })
        in_maps.append(m)
    nc = build_program(NT)
    res = run_bass_kernel_spmd(nc, in_maps, core_ids=list(range(NCORES)))
    R = res.results
    if _raw:
        return R
    NS = NT // 8
    ntok = NT * 128
    yp = np.zeros((1, 16384, D), np.float32)
    fk = np.zeros((1, 16384, 16, 64), np.float32)
    fv = np.zeros((1, 16384, 16, 64), np.float32)
    fl = np.zeros((1, 16384, 16), np.float32)
    for c in range(NCORES):
        for j in range(NS):
            r = 8 * j + c
            yp[0, r * 128:(r + 1) * 128] = np.asarray(R[c]["y_o"])[j]
            fk[0, r * 128:(r + 1) * 128] = np.asarray(R[c]["fk_o"])[j].reshape(128, 16, 64)
            fv[0, r * 128:(r + 1) * 128] = np.asarray(R[c]["fv_o"])[j].reshape(128, 16, 64)
            fl[0, r * 128:(r + 1) * 128] = np.asarray(R[c]["fl_o"])[j]
    pool_p = np.asarray(R[7]["pool_o"]).reshape(1, 15, 512).astype(np.float32)
    swk_p = np.asarray(R[7]["swk_o"]).reshape(1, 128, 2, 64).astype(np.float32)
    swv_p = np.asarray(R[7]["swv_o"]).reshape(1, 128, 2, 64).astype(np.float32)
    cat = lambda k: np.concatenate([np.asarray(R[c][k], dtype=np.float32) for c in range(NCORES)], axis=0)
    ys = cat("ys_o")
    pool_s = cat("pools_o")
    swk_s = cat("swks_o").reshape(16, 128, 2, 64)
    swv_s = cat("swvs_o").reshape(16, 128, 2, 64)
    fk_s = cat("fks_o").reshape(16, 32, 16, 64)
    fv_s = cat("fvs_o").reshape(16, 32, 16, 64)
    fl_s = cat("fls_o")
    return (yp, ys, pool_p, pool_s, swk_p, swv_p, swk_s, swv_s, fk, fv, fl, fk_s, fv_s, fl_s)
```
